# Optimizing a Trainium2 kernel written in Bass

```python
import jax, jax.numpy as jnp
from jax import lax
import numpy as np

D_MODEL = 2048
BATCH = 4
SEQ = 4096
DEPTH = 2

GRID_W = 64
CTX_LEN = 256
N_MIXERS = 2
N_HEADS = 16
HEAD_DIM = D_MODEL // N_HEADS
WIN_H = 8
WIN_W = 16
CHUNK = 128
N_GROUPS = 16
GROUP_DIM = D_MODEL // N_GROUPS
D_FF = -(-(8 * D_MODEL) // (3 * 256)) * 256
N_NA = (DEPTH + 1) // 2
N_GM = DEPTH // 2
ALPHA = (2 * DEPTH) ** 0.25
BETA = (8 * DEPTH) ** -0.25
LN_EPS = 1e-5

kernel_name = "hybrid_natten_gmlp_deepnorm_dit"


def layer_norm(x, g, b):
    xf = x.astype(jnp.float32)
    mu = jnp.mean(xf, axis=-1, keepdims=True)
    var = jnp.mean(jnp.square(xf - mu), axis=-1, keepdims=True)
    return ((xf - mu) * lax.rsqrt(var + LN_EPS)).astype(x.dtype) * g + b


def ada_params(cond, w, b):
    return jnp.split(jax.nn.silu(cond) @ w + b, 6, axis=-1)


def modulate(h, shift, scale):
    return h * (1 + scale) + shift


def swiglu(h, w_in, w_out):
    gate, up = jnp.split(h @ w_in, 2, axis=-1)
    return (jax.nn.silu(gate) * up) @ w_out


def to_heads(t):
    b, n, _ = t.shape
    return jnp.transpose(t.reshape(b, n, N_HEADS, HEAD_DIM), (0, 2, 1, 3))


def neighbourhood_attention(q, k, v, ctx_k, ctx_v, rpb):
    bsz, nh, rows, width, hd = q.shape
    kh, kw = min(WIN_H, rows), min(WIN_W, width)
    col = np.arange(width)
    col_idx = np.clip(col - kw // 2, 0, width - kw)[:, None] + np.arange(kw)[None, :]
    col_rel = col_idx - col[:, None] + (WIN_W - 1)
    scale = hd ** -0.5

    def one_row(args):
        q_r, r = args
        r0 = jnp.clip(r - kh // 2, 0, rows - kh)
        k_win = jnp.take(lax.dynamic_slice_in_dim(k, r0, kh, axis=2), col_idx, axis=3)
        v_win = jnp.take(lax.dynamic_slice_in_dim(v, r0, kh, axis=2), col_idx, axis=3)
        row_rel = r0 + jnp.arange(kh) - r + (WIN_H - 1)
        bias = rpb[:, row_rel[:, None, None], col_rel[None, :, :]]
        s_win = (jnp.einsum("bhcd,bhicjd->bhcij", q_r, k_win) * scale
                 + jnp.transpose(bias, (0, 2, 1, 3)))
        s_ctx = jnp.einsum("bhcd,bhld->bhcl", q_r, ctx_k) * scale
        s = jnp.concatenate([s_win.reshape(bsz, nh, width, kh * kw), s_ctx], axis=-1)
        p = jax.nn.softmax(s.astype(jnp.float32), axis=-1).astype(v.dtype)
        p_win = p[..., :kh * kw].reshape(bsz, nh, width, kh, kw)
        return (jnp.einsum("bhcij,bhicjd->bhcd", p_win, v_win)
                + jnp.einsum("bhcl,bhld->bhcd", p[..., kh * kw:], ctx_v))

    out = lax.map(one_row, (jnp.moveaxis(q, 2, 0), jnp.arange(rows)))
    return jnp.moveaxis(out, 0, 2)


def na_mixer(a, ac, w_qkv, w_o, rpb, rows, ctx_queries):
    bsz, n, _ = a.shape
    q, k, v = jnp.split(a @ w_qkv, 3, axis=-1)

    def to_grid(t):
        return jnp.transpose(t.reshape(bsz, rows, GRID_W, N_HEADS, HEAD_DIM), (0, 3, 1, 2, 4))

    if ctx_queries:
        qc, kc, vc = jnp.split(ac @ w_qkv, 3, axis=-1)
    else:
        kc, vc = jnp.split(ac @ w_qkv[:, D_MODEL:], 2, axis=-1)
    kc_h, vc_h = to_heads(kc), to_heads(vc)
    o = neighbourhood_attention(to_grid(q), to_grid(k), to_grid(v), kc_h, vc_h, rpb)
    y = jnp.transpose(o, (0, 2, 3, 1, 4)).reshape(bsz, n, D_MODEL) @ w_o
    yc = None
    if ctx_queries:
        s = jnp.einsum("bhld,bhmd->bhlm", to_heads(qc), kc_h) * HEAD_DIM ** -0.5
        p = jax.nn.softmax(s.astype(jnp.float32), axis=-1).astype(vc_h.dtype)
        yc = jnp.einsum("bhlm,bhmd->blhd", p, vc_h).reshape(ac.shape) @ w_o
    return y, yc


def spatial_gating_mlp(a, w_in, ln_g, ln_b, w_s, b_s, w_out):
    bsz, n, _ = a.shape
    u, v = jnp.split(jax.nn.gelu(a @ w_in, approximate=False), 2, axis=-1)
    v = layer_norm(v, ln_g, ln_b).reshape(bsz, n // CHUNK, CHUNK, N_GROUPS, GROUP_DIM)
    v = jnp.einsum("gpq,bkqgc->bkpgc", w_s, v) + jnp.transpose(b_s)[:, :, None]
    return (u * v.reshape(bsz, n, D_MODEL)) @ w_out


def setup_inputs(seed: int = 0) -> dict:
    key = jax.random.key(seed)
    ks = jax.random.split(key, 19)
    nrm = jax.random.normal
    f32 = jnp.float32
    D = D_MODEL
    return {
        "x": nrm(ks[0], (BATCH, SEQ, D), f32),
        "c": nrm(ks[1], (BATCH, D), f32),
        "ctx": nrm(ks[2], (BATCH, CTX_LEN, D), f32),
        "c_ctx": nrm(ks[3], (D,), f32),
        "ada_w": nrm(ks[4], (DEPTH, D, 6 * D), f32) * D ** -0.5,
        "ada_b": nrm(ks[5], (DEPTH, 6 * D), f32) * 0.01,
        "ln_g": 1.0 + 0.02 * nrm(ks[6], (DEPTH, 2, D), f32),
        "ln_b": 0.02 * nrm(ks[7], (DEPTH, 2, D), f32),
        "na_w_qkv": nrm(ks[8], (N_NA, D, 3 * D), f32) * D ** -0.5,
        "na_w_o": nrm(ks[9], (N_NA, D, D), f32) * (D ** -0.5 * BETA),
        "na_rpb": 0.1 * nrm(ks[10], (N_NA, N_HEADS, 2 * WIN_H - 1, 2 * WIN_W - 1), f32),
        "gm_w_in": nrm(ks[11], (N_GM, D, 2 * D), f32) * D ** -0.5,
        "gm_ln_g": 1.0 + 0.02 * nrm(ks[12], (N_GM, D), f32),
        "gm_ln_b": 0.02 * nrm(ks[13], (N_GM, D), f32),
        "gm_w_s": nrm(ks[14], (N_GM, N_GROUPS, CHUNK, CHUNK), f32) * CHUNK ** -0.5,
        "gm_b_s": 0.02 * nrm(ks[15], (N_GM, N_GROUPS, CHUNK), f32),
        "gm_w_out": nrm(ks[16], (N_GM, D, D), f32) * (D ** -0.5 * BETA),
        "ffn_w_in": nrm(ks[17], (DEPTH, D, 2 * D_FF), f32) * D ** -0.5,
        "ffn_w_out": nrm(ks[18], (DEPTH, D_FF, D), f32) * (D_FF ** -0.5 * BETA),
    }


def reference(x, c, ctx, c_ctx, ada_w, ada_b, ln_g, ln_b, na_w_qkv, na_w_o, na_rpb,
              gm_w_in, gm_ln_g, gm_ln_b, gm_w_s, gm_b_s, gm_w_out, ffn_w_in, ffn_w_out):
    rows = x.shape[1] // GRID_W
    h, hc = x, ctx
    for i in range(DEPTH):
        kind = i % N_MIXERS
        j = i // N_MIXERS
        ctx_out = any(l % N_MIXERS == 0 for l in range(i + 1, DEPTH))
        sh1, sc1, g1, sh2, sc2, g2 = ada_params(c[:, None, :], ada_w[i], ada_b[i])
        a = modulate(h, sh1, sc1)
        if kind == 0 or ctx_out:
            csh1, csc1, cg1, csh2, csc2, cg2 = ada_params(c_ctx, ada_w[i], ada_b[i])
            ac = modulate(hc, csh1, csc1)
        if kind == 0:
            y, yc = na_mixer(a, ac, na_w_qkv[j], na_w_o[j], na_rpb[j], rows, ctx_out)
        else:
            gm = (gm_w_in[j], gm_ln_g[j], gm_ln_b[j], gm_w_s[j], gm_b_s[j], gm_w_out[j])
            y = spatial_gating_mlp(a, *gm)
            if ctx_out:
                yc = spatial_gating_mlp(ac, *gm)
        h = layer_norm(ALPHA * h + g1 * y, ln_g[i, 0], ln_b[i, 0])
        h = layer_norm(ALPHA * h + g2 * swiglu(modulate(h, sh2, sc2), ffn_w_in[i], ffn_w_out[i]),
                       ln_g[i, 1], ln_b[i, 1])
        if ctx_out:
            hc = layer_norm(ALPHA * hc + cg1 * yc, ln_g[i, 0], ln_b[i, 0])
            hc = layer_norm(ALPHA * hc + cg2 * swiglu(modulate(hc, csh2, csc2), ffn_w_in[i], ffn_w_out[i]),
                            ln_g[i, 1], ln_b[i, 1])
    return h
```

```python
import numpy as np
import concourse.bass as bass
import concourse.mybir as mybir
from concourse.bass_utils import run_bass_kernel_spmd

F32 = mybir.dt.float32
BF16 = mybir.dt.bfloat16
AF = mybir.ActivationFunctionType
ALU = mybir.AluOpType

D = 2048
KC = 16
T = 2048
TE = 2304
NCTX = 256
NH = 16
DFF = 5632
FC = 44
DEPTH = 2
ALPHA = (2 * DEPTH) ** 0.25
LN_EPS = 1e-5
EPS_RES = LN_EPS / (ALPHA * ALPHA)
MASKV = -30000.0
NPAT = 3

ENGS = ("sp", "act", "pe", "dve", "pool")
DECL = []


class Buf:
    __slots__ = ("name", "last_w", "reads", "dsem", "ndma")

    def __init__(self, name):
        self.name = name
        self.last_w = None
        self.reads = []
        self.dsem = None
        self.ndma = 0


class Op:
    __slots__ = ("eng", "fn", "deps", "dma", "sig", "val", "sem")

    def __init__(self, eng, fn):
        self.eng = eng
        self.fn = fn
        self.deps = []
        self.dma = None
        self.sig = False
        self.val = 0
        self.sem = None


class Prog:
    def __init__(self, nc):
        self.nc = nc
        self.ops = {e: [] for e in ENGS}
        self.nsem = 0

    def newsem(self, name):
        self.nsem += 1
        return self.nc.alloc_semaphore(name)

    def add(self, eng, fn, reads=(), writes=(), dma=None):
        op = Op(eng, fn)
        deps = []
        for b in reads:
            if b.last_w is not None:
                deps.append(b.last_w)
        for b in writes:
            if b.last_w is not None:
                deps.append(b.last_w)
            deps.extend(b.reads)
        seen = set()
        for d in deps:
            if id(d) in seen or d is op:
                continue
            seen.add(id(d))
            if d.eng == "pe" and eng == "pe" and d.dma is None:
                continue
            op.deps.append(d)
            d.sig = True
        for b in reads:
            b.reads.append(op)
        for b in writes:
            b.last_w = op
            b.reads = []
        if dma is not None:
            op.dma = dma
            if dma.dsem is None:
                dma.dsem = self.newsem("d_" + dma.name)
            dma.ndma += 1
            op.sem = dma.dsem
            op.val = 16 * dma.ndma
        self.ops[eng].append(op)
        return op

    def emit(self, block):
        for e in ENGS:
            cnt = 0
            sem = None
            for op in self.ops[e]:
                if op.dma is not None:
                    continue
                if op.sig:
                    if sem is None:
                        sem = self.newsem("e_" + e)
                    cnt += 1
                    op.sem = sem
                    op.val = cnt
        P = self

        def run(e, eng):
            waited = {}
            for op in P.ops[e]:
                for d in op.deps:
                    key = d.sem.num
                    if waited.get(key, 0) >= d.val:
                        continue
                    eng.wait_ge(d.sem, d.val)
                    waited[key] = d.val
                ins = op.fn(eng)
                if op.dma is not None:
                    ins.then_inc(op.sem, 16)
                elif op.sig:
                    ins.then_inc(op.sem, 1)
            fin = {}
            for op in P.ops[e]:
                if op.dma is not None:
                    prev = fin.get(op.sem.num, (None, 0))[1]
                    fin[op.sem.num] = (op.sem, max(prev, op.val))
            for sem, v in fin.values():
                if waited.get(sem.num, 0) < v:
                    eng.wait_ge(sem, v)

        block.sync(lambda eng: run("sp", eng))
        block.scalar(lambda eng: run("act", eng))
        block.tensor(lambda eng: run("pe", eng))
        block.vector(lambda eng: run("dve", eng))
        block.gpsimd(lambda eng: run("pool", eng))


class Arena:
    def __init__(self, nc, name, nelem, dtype, unit):
        self.t = nc.alloc_sbuf_tensor(name, [128, nelem], dtype)
        self.unit = unit
        self.bufs = [Buf(f"{name}{i}") for i in range((nelem + unit - 1) // unit)]
        self.nelem = nelem

    def u(self, lo, hi):
        return self.bufs[lo // self.unit:(hi - 1) // self.unit + 1]


class View:
    def __init__(self, arena, off, a, b):
        self.ar = arena
        self.off = off
        self.a = a
        self.b = b
        assert off + a * b <= arena.nelem, (off, a, b, arena.nelem)

    def ap(self, i, lo=0, hi=None):
        hi = self.b if hi is None else hi
        s = self.off + i * self.b
        return self.ar.t[:, s + lo:s + hi]

    def ap3(self, i0, i1, lo=0, hi=None):
        hi = self.b if hi is None else hi
        full = self.ar.t[:, self.off:self.off + self.a * self.b].rearrange("p (a b) -> p a b", b=self.b)
        return full[:, i0:i1, lo:hi]

    def bufs(self, i, lo=0, hi=None):
        hi = self.b if hi is None else hi
        s = self.off + i * self.b
        return self.ar.u(s + lo, s + hi)

    def bufs3(self, i0, i1, lo=0, hi=None):
        out = []
        for i in range(i0, i1):
            for b in self.bufs(i, lo, hi):
                if not out or out[-1] is not b:
                    out.append(b)
        return out


def build(stage=99, debug=False):
    nc = bass.Bass("TRN2", target_bir_lowering=False)
    P = Prog(nc)

    import os
    SKIP_ADA = debug and os.environ.get("K_SKIPADA") == "1"
    NPAIR = int(os.environ.get("K_NPAIR", "8")) if debug else 8
    ATT = int(os.environ.get("K_ATT", "1")) if debug else 1
    NI = int(os.environ.get("K_NI", "16")) if debug else 16
    DECL.clear()

    def din(name, shape, dt=F32, need=True):
        if not need:
            return None
        DECL.append(name)
        return nc.dram_tensor(name, list(shape), dt, kind="ExternalInput").ap()

    def dscr(name, shape, dt):
        kind = "ExternalOutput" if debug else "Internal"
        return nc.dram_tensor(name, list(shape), dt, kind=kind).ap()

    xT = din("xT", [D, TE])
    ctxT = din("ctxT", [D, NCTX])
    cvec = din("cvec", [128, KC * 2])
    ada_w = din("ada_w", [DEPTH, D, 6 * D], need=not SKIP_ADA)
    adab = din("adab", [128, DEPTH * 96 * 2])
    lnp = din("lnp", [128, DEPTH * 2 * 2 * KC])
    w_qkv = din("w_qkv", [D, 3 * D])
    w_o = din("w_o", [D, D])
    biasT = din("biasT", [NH, 128, NPAT * 5 * 128])
    gm_w_in = din("gm_w_in", [D, 2 * D], need=stage >= 7)
    gmln = din("gmln", [128, 2 * KC])
    w_sT = din("w_sT", [128, KC * 128])
    bsb = din("bsb", [128, KC * 128])
    identin = din("identin", [128, 128])
    gm_w_out = din("gm_w_out", [D, D], need=stage >= 9)
    ffn_w_in = din("ffn_w_in", [DEPTH, D, 2 * DFF], need=stage >= 5)
    ffn_w_out = din("ffn_w_out", [DEPTH, DFF, D], need=stage >= 5)
    outT = nc.dram_tensor("outT", [D, T], F32, kind="ExternalOutput").ap()

    OTs = dscr("OTs", [D, T], BF16)
    Hpre = dscr("Hpre", [D, T], F32)
    Hres = dscr("Hres", [D, T], F32)
    Gs = dscr("Gs", [DFF, T], BF16)
    Vs = dscr("Vs", [D, T], F32)

    def dgrid(name, nch):
        return [[Buf(f"{name}_{c}_{t}") for t in range(4)] for c in range(nch)]

    B_OTs = dgrid("OTs", KC)
    B_Hpre = dgrid("Hpre", KC)
    B_Hres = dgrid("Hres", KC)
    B_Gs = dgrid("Gs", FC)
    B_Vs = dgrid("Vs", KC)
    B_out = dgrid("out", KC)

    A = Arena(nc, "arA", 45056, BF16, 512)
    Bq = Arena(nc, "arB", 32768, BF16, 512)
    Wa = Arena(nc, "arW", 16384, BF16, 4096)
    NWK = 5
    wk = [nc.alloc_sbuf_tensor(f"wk{i}", [128, 512], F32) for i in range(NWK)]
    wkB = [Buf(f"wk{i}") for i in range(NWK)]
    NHB = 3
    hb = [nc.alloc_sbuf_tensor(f"hb{i}", [128, 512], BF16) for i in range(NHB)]
    hbB = [Buf(f"hb{i}") for i in range(NHB)]
    ident = nc.alloc_sbuf_tensor("ident", [128, 128], BF16)
    ones = nc.alloc_sbuf_tensor("ones", [128, 128], BF16)
    onesf = nc.alloc_sbuf_tensor("onesf", [128, 128], F32)
    i2f = nc.alloc_sbuf_tensor("i2f", [128, 2], F32)
    B_const = Buf("const")
    adaP = nc.alloc_sbuf_tensor("adaP", [128, DEPTH * 96 * 2], F32)
    B_adaP = Buf("adaP")
    adabs = nc.alloc_sbuf_tensor("adabs", [128, DEPTH * 96 * 2], F32)
    B_adabs = Buf("adabs")
    lnps = nc.alloc_sbuf_tensor("lnps", [128, DEPTH * 2 * 2 * KC], F32)
    gmlns = nc.alloc_sbuf_tensor("gmlns", [128, 2 * KC], F32)
    cv = nc.alloc_sbuf_tensor("cv", [128, KC * 2], F32)
    scb = nc.alloc_sbuf_tensor("scb", [128, KC * 2], BF16)
    B_small = Buf("small")
    NDER = 20
    der = nc.alloc_sbuf_tensor("der", [128, NDER * KC], F32)
    B_der = Buf("der")
    epsT = nc.alloc_sbuf_tensor("epsT", [128, 2], F32)
    st = nc.alloc_sbuf_tensor("st", [128, 2 * 512], F32)
    B_st = [Buf(f"st{i}") for i in range(2)]

    ps = [nc.alloc_psum_tensor(f"ps{i}", [128, 512], F32) for i in range(8)]
    psB = [Buf(f"ps{i}") for i in range(8)]

    wrr = [0]

    def wslot():
        s = wrr[0] % NWK
        wrr[0] += 1
        return s

    hrr = [0]

    def hslot():
        s = hrr[0] % NHB
        hrr[0] += 1
        return s

    P.add("pool", lambda e: e.dma_start(out=ident[:], in_=identin), writes=[B_const], dma=B_const)
    P.add("dve", lambda e: e.memset(ones[:], 1.0), writes=[B_const])
    P.add("dve", lambda e: e.memset(onesf[:], 1.0), writes=[B_const])
    P.add("act", lambda e: e.activation(out=i2f[0:2, 0:2], in_=ident[0:2, 0:2], func=AF.Copy),
          reads=[B_const], writes=[B_const])
    P.add("dve", lambda e: e.memset(der[:, :], 0.0), writes=[B_der])
    P.add("dve", lambda e: e.memset(epsT[:, 0:1], EPS_RES), writes=[B_const])
    P.add("dve", lambda e: e.memset(epsT[:, 1:2], LN_EPS), writes=[B_const])
    P.add("sp", lambda e: e.dma_start(out=cv[:], in_=cvec), writes=[B_small], dma=B_small)
    P.add("sp", lambda e: e.dma_start(out=lnps[:], in_=lnp), writes=[B_small], dma=B_small)
    P.add("sp", lambda e: e.dma_start(out=gmlns[:], in_=gmln), writes=[B_small], dma=B_small)
    P.add("sp", lambda e: e.dma_start(out=adabs[:, :], in_=adab), writes=[B_adabs], dma=B_adabs)
    P.add("act", lambda e: e.activation(out=scb[:], in_=cv[:], func=AF.Silu), reads=[B_small], writes=[B_small])

    wq = [0]

    def wload(src_ap, ncols, nk):
        nelem = nk * ncols
        nunits = (nelem + 4095) // 4096
        assert nunits in (1, 2, 4)
        u0 = (wq[0] % (4 // nunits)) * nunits
        wq[0] += 1
        v = View(Wa, u0 * 4096, nk, ncols)
        bl = Wa.bufs[u0:u0 + nunits]
        dst = Wa.t[:, u0 * 4096:u0 * 4096 + nelem].rearrange("p (k n) -> p k n", n=ncols)
        src = src_ap.rearrange("(k p) n -> p k n", p=128)
        P.add("pool", lambda e: e.dma_start(out=dst, in_=src), writes=bl, dma=bl[0])
        return v

    def der_ap(i, j):
        return der[:, i * KC + j:i * KC + j + 1]

    def der_row(i):
        return der[:, i * KC:(i + 1) * KC]

    def ada_ap(l, g, which):
        v = adaP[:, :].rearrange("p (l j w) -> p l j w", l=DEPTH, w=2)
        return v[:, l, g * KC:(g + 1) * KC, which]

    def lnp_ap(l, wh, gb):
        v = lnps[:, :].rearrange("p (l a g k) -> p l a g k", l=DEPTH, a=2, g=2)
        return v[:, l, wh, gb, :]

    ada_next = [0]
    ada_pend = [None]

    def ada_finish():
        pend = ada_pend[0]
        ada_pend[0] = None
        if pend is None:
            return
        l, blk, s, psb = pend
        for jj in range(4):
            P.add("pe", lambda e, jj=jj: e.matmul(ps[psb][:, jj * 2:jj * 2 + 2],
                                                  lhsT=wk[s][0:2, jj * 128:(jj + 1) * 128], rhs=i2f[0:2, 0:2],
                                                  start=True, stop=True),
                  reads=[wkB[s], B_const], writes=[psB[psb]])
        o = l * 192 + blk * 8
        P.add("dve", lambda e: e.tensor_tensor(out=adaP[:, o:o + 8], in0=ps[psb][:, 0:8],
                                               in1=adabs[:, o:o + 8], op=ALU.add),
              reads=[psB[psb], B_adabs], writes=[B_adaP])

    def ada_block():
        gb = ada_next[0]
        ada_next[0] += 1
        if gb >= 48 or SKIP_ADA:
            return
        l, blk = gb // 24, gb % 24
        psb = 6 + gb % 2
        v = wload(ada_w[l, :, blk * 512:(blk + 1) * 512], 512, KC)
        for k in range(KC):
            P.add("pe", lambda e, k=k: e.matmul(ps[psb][0:2, 0:512], lhsT=scb[:, k * 2:k * 2 + 2], rhs=v.ap(k),
                                               start=(k == 0), stop=(k == KC - 1)),
                  reads=v.bufs(k) + [B_small], writes=[psB[psb]])
        s = wslot()
        P.add("dve", lambda e: e.tensor_copy(out=wk[s][0:2, :], in_=ps[psb][0:2, 0:512]),
              reads=[psB[psb]], writes=[wkB[s]])
        ada_finish()
        ada_pend[0] = (l, blk, s, psb)

    if SKIP_ADA:
        P.add("dve", lambda e: e.memset(adaP[:, :], 0.25), writes=[B_adaP])
    for _ in range(8):
        ada_block()
    ada_finish()

    DER = {}

    def dalloc(name):
        DER[name] = len(DER)
        assert len(DER) <= NDER
        return DER[name]

    def dve_der(fn):
        P.add("dve", fn, reads=[B_adaP, B_small, B_der], writes=[B_der])

    i = dalloc("S1_0")
    dve_der(lambda e, i=i: e.tensor_scalar(out=der_row(i), in0=ada_ap(0, 1, 0), scalar1=1.0, scalar2=None,
                                           op0=ALU.add))
    i = dalloc("B1_0")
    dve_der(lambda e, i=i: e.tensor_copy(out=der_row(i), in_=ada_ap(0, 0, 0)))
    i = dalloc("cS1")
    dve_der(lambda e, i=i: e.tensor_scalar(out=der_row(i), in0=ada_ap(0, 1, 1), scalar1=1.0, scalar2=None,
                                           op0=ALU.add))
    i = dalloc("cB1")
    dve_der(lambda e, i=i: e.tensor_copy(out=der_row(i), in_=ada_ap(0, 0, 1)))

    def late_derived():
        for l in range(DEPTH):
            if l > 0:
                i = dalloc(f"S1_{l}")
                dve_der(lambda e, i=i, l=l: e.tensor_scalar(out=der_row(i), in0=ada_ap(l, 1, 0), scalar1=1.0,
                                                            scalar2=None, op0=ALU.add))
            i = dalloc(f"g1_{l}")
            dve_der(lambda e, i=i, l=l: e.tensor_scalar(out=der_row(i), in0=ada_ap(l, 2, 0), scalar1=1.0 / ALPHA,
                                                        scalar2=None, op0=ALU.mult))
            i = dalloc(f"g2_{l}")
            dve_der(lambda e, i=i, l=l: e.tensor_scalar(out=der_row(i), in0=ada_ap(l, 5, 0), scalar1=1.0 / ALPHA,
                                                        scalar2=None, op0=ALU.mult))
            i = dalloc(f"S2_{l}")
            dve_der(lambda e, i=i, l=l: e.tensor_scalar(out=der_row(i), in0=ada_ap(l, 4, 0), scalar1=1.0,
                                                        scalar2=None, op0=ALU.add))

        def mod_after_ln(name, l_ln, wh, sname, shift_l, shift_g):
            ia = dalloc(name + "_a")
            ib = dalloc(name + "_b")
            dve_der(lambda e: e.tensor_tensor(out=der_row(ia), in0=lnp_ap(l_ln, wh, 0), in1=der_row(DER[sname]),
                                              op=ALU.mult))
            dve_der(lambda e: e.tensor_tensor(out=der_row(ib), in0=lnp_ap(l_ln, wh, 1), in1=der_row(DER[sname]),
                                              op=ALU.mult))
            dve_der(lambda e: e.tensor_tensor(out=der_row(ib), in0=der_row(ib), in1=ada_ap(shift_l, shift_g, 0),
                                              op=ALU.add))

        mod_after_ln("m00", 0, 0, "S2_0", 0, 3)
        mod_after_ln("m01", 0, 1, "S1_1", 1, 0)
        mod_after_ln("m10", 1, 0, "S2_1", 1, 3)

    if stage <= 0:
        return finish(nc, P)

    gemm_rr = [0]

    def gemm_group(pb, Aview, wv, col0, tok0, ntok, kchunks=None, koff=0, start=True, stop=True):
        nk = wv.a if kchunks is None else kchunks
        for k in range(nk):
            P.add("pe", lambda e, k=k: e.matmul(ps[pb][:, 0:ntok], lhsT=wv.ap(k, col0, col0 + 128),
                                                rhs=Aview.ap(k + koff, tok0, tok0 + ntok),
                                                start=(start and k == 0), stop=(stop and k == nk - 1)),
                  reads=wv.bufs(k, col0, col0 + 128) + Aview.bufs(k + koff, tok0, tok0 + ntok),
                  writes=[psB[pb]])

    aT = View(A, 0, KC, TE)
    acT = View(Bq, 0, KC, NCTX)
    for k in range(KC):
        for t0 in range(0, TE, 512):
            n = min(512, TE - t0)
            s = wslot()
            P.add("sp", lambda e, s=s, k=k, t0=t0, n=n: e.dma_start(out=wk[s][:, 0:n],
                                                                  in_=xT[k * 128:(k + 1) * 128, t0:t0 + n]),
                  writes=[wkB[s]], dma=wkB[s])
            P.add("act", lambda e, s=s, k=k, t0=t0, n=n: e.activation(
                out=aT.ap(k, t0, t0 + n), in_=wk[s][:, 0:n], func=AF.Identity,
                bias=der_ap(DER["B1_0"], k), scale=der_ap(DER["S1_0"], k)),
                reads=[wkB[s], B_der], writes=aT.bufs(k, t0, t0 + n))
        s = wslot()
        P.add("sp", lambda e, s=s, k=k: e.dma_start(out=wk[s][:, 0:NCTX], in_=ctxT[k * 128:(k + 1) * 128, :]),
              writes=[wkB[s]], dma=wkB[s])
        P.add("act", lambda e, s=s, k=k: e.activation(
            out=acT.ap(k), in_=wk[s][:, 0:NCTX], func=AF.Identity,
            bias=der_ap(DER["cB1"], k), scale=der_ap(DER["cS1"], k)),
            reads=[wkB[s], B_der], writes=acT.bufs(k))

    if stage <= 1:
        return finish(nc, P)
    o = 4096
    qT = View(Bq, o, 2, T); o += 2 * T
    kT = View(Bq, o, 2, TE); o += 2 * TE
    kcT = View(Bq, o, 2, NCTX); o += 2 * NCTX
    Vv = View(Bq, o, 18, 256); o += 18 * 256
    Vc = View(Bq, o, 2, 256); o += 2 * 256
    bia = [View(Bq, o + i * NPAT * 640, NPAT * 5, 128) for i in range(2)]; o += 2 * NPAT * 640
    PT = [View(Bq, o + i * 896, 7, 128) for i in range(2)]; o += 2 * 896
    OTst = [View(Bq, o + i * T, 1, T) for i in range(2)]; o += 2 * T
    assert o <= 32768, o
    SCALE = 128 ** -0.5

    def evac(pb, n, dst_ap, dst_bufs, scale=None, eng="dve"):
        if eng == "dve":
            if scale is None:
                P.add("dve", lambda e: e.tensor_copy(out=dst_ap, in_=ps[pb][:, 0:n]),
                      reads=[psB[pb]], writes=dst_bufs)
            else:
                P.add("dve", lambda e: e.tensor_scalar(out=dst_ap, in0=ps[pb][:, 0:n], scalar1=scale,
                                                       scalar2=None, op0=ALU.mult),
                      reads=[psB[pb]], writes=dst_bufs)
        else:
            if scale is None:
                P.add("act", lambda e: e.activation(out=dst_ap, in_=ps[pb][:, 0:n], func=AF.Copy),
                      reads=[psB[pb]], writes=dst_bufs)
            else:
                P.add("act", lambda e: e.activation(out=dst_ap, in_=ps[pb][:, 0:n], func=AF.Identity,
                                                    scale=scale),
                      reads=[psB[pb]], writes=dst_bufs)

    def attention_pair(hp):
        wqv = wload(w_qkv[:, hp * 256:(hp + 1) * 256], 256, KC)
        wkv = wload(w_qkv[:, D + hp * 256:D + (hp + 1) * 256], 256, KC)
        wvv = wload(w_qkv[:, 2 * D + hp * 256:2 * D + (hp + 1) * 256], 256, KC)
        for hh in range(2):
            h = hp * 2 + hh
            bl = bia[hh].bufs3(0, NPAT * 5)
            P.add("pool", lambda e, hh=hh, h=h: e.dma_start(
                out=Bq.t[:, bia[hh].off:bia[hh].off + NPAT * 640], in_=biasT[h]), writes=bl, dma=bl[0])
        pbs = (6, 7)
        g = 0
        for hh in range(2):
            for t0 in range(0, T, 512):
                pb = pbs[g % 2]; g += 1
                gemm_group(pb, aT, wqv, hh * 128, t0, 512)
                evac(pb, 512, qT.ap(hh, t0, t0 + 512), qT.bufs(hh, t0, t0 + 512), scale=SCALE,
                     eng=("act" if g % 2 else "dve"))
            for t0 in range(0, TE, 512):
                n = min(512, TE - t0)
                pb = pbs[g % 2]; g += 1
                gemm_group(pb, aT, wkv, hh * 128, t0, n)
                evac(pb, n, kT.ap(hh, t0, t0 + n), kT.bufs(hh, t0, t0 + n), eng=("act" if g % 2 else "dve"))
            pb = pbs[g % 2]; g += 1
            gemm_group(pb, acT, wkv, hh * 128, 0, NCTX)
            evac(pb, NCTX, kcT.ap(hh), kcT.bufs(hh), eng=("act" if g % 2 else "dve"))
        for tb in range(18 + 2):
            pb = pbs[g % 2]; g += 1
            for k in range(KC):
                if tb < 18:
                    lhs = aT.ap(k, tb * 128, tb * 128 + 128)
                    lb = aT.bufs(k, tb * 128, tb * 128 + 128)
                else:
                    lhs = acT.ap(k, (tb - 18) * 128, (tb - 18) * 128 + 128)
                    lb = acT.bufs(k, (tb - 18) * 128, (tb - 18) * 128 + 128)
                P.add("pe", lambda e, k=k, lhs=lhs, pb=pb: e.matmul(ps[pb][:, 0:256], lhsT=lhs, rhs=wvv.ap(k),
                                                                   start=(k == 0), stop=(k == KC - 1)),
                      reads=lb + wvv.bufs(k), writes=[psB[pb]])
            if tb < 18:
                evac(pb, 256, Vv.ap(tb), Vv.bufs(tb), eng=("act" if g % 2 else "dve"))
            else:
                evac(pb, 256, Vc.ap(tb - 18), Vc.bufs(tb - 18), eng=("act" if g % 2 else "dve"))

        for hh in range(2):
            h = hp * 2 + hh

            def qk(i, sset, hh=hh):
                kb0 = max(i - 2, 0)
                pat = min(i, NPAT - 1)
                for blk in range(7):
                    pb = sset * 2 + (0 if blk < 4 else 1)
                    c0 = (blk % 4) * 128
                    if blk < 5:
                        kt0 = (kb0 + blk) * 128
                        P.add("pe", lambda e, pb=pb, c0=c0, kt0=kt0: e.matmul(
                            ps[pb][:, c0:c0 + 128], lhsT=kT.ap(hh, kt0, kt0 + 128),
                            rhs=qT.ap(hh, i * 128, i * 128 + 128), start=True, stop=False),
                            reads=kT.bufs(hh, kt0, kt0 + 128) + qT.bufs(hh, i * 128, i * 128 + 128),
                            writes=[psB[pb]])
                        P.add("pe", lambda e, pb=pb, c0=c0, blk=blk, pat=pat: e.matmul(
                            ps[pb][:, c0:c0 + 128], lhsT=ident[:, :], rhs=bia[hh].ap(pat * 5 + blk),
                            start=False, stop=True),
                            reads=bia[hh].bufs(pat * 5 + blk) + [B_const], writes=[psB[pb]])
                    else:
                        cb = blk - 5
                        P.add("pe", lambda e, pb=pb, c0=c0, cb=cb: e.matmul(
                            ps[pb][:, c0:c0 + 128], lhsT=kcT.ap(hh, cb * 128, cb * 128 + 128),
                            rhs=qT.ap(hh, i * 128, i * 128 + 128), start=True, stop=True),
                            reads=kcT.bufs(hh, cb * 128, cb * 128 + 128) + qT.bufs(hh, i * 128, i * 128 + 128),
                            writes=[psB[pb]])

            def rest(i, sset, hh=hh):
                kb0 = max(i - 2, 0)
                pt = PT[sset]
                P.add("act", lambda e: e.activation(out=Bq.t[:, pt.off:pt.off + 512], in_=ps[sset * 2][:, 0:512],
                                                    func=AF.Exp),
                      reads=[psB[sset * 2]], writes=pt.bufs3(0, 4))
                P.add("act", lambda e: e.activation(out=Bq.t[:, pt.off + 512:pt.off + 896],
                                                    in_=ps[sset * 2 + 1][:, 0:384], func=AF.Exp),
                      reads=[psB[sset * 2 + 1]], writes=pt.bufs3(4, 7))
                half = i % 2
                pbo = 4 + half
                for blk in range(7):
                    if blk < 5:
                        vap = Vv.ap(kb0 + blk, hh * 128, hh * 128 + 128)
                        vb = Vv.bufs(kb0 + blk, hh * 128, hh * 128 + 128)
                    else:
                        vap = Vc.ap(blk - 5, hh * 128, hh * 128 + 128)
                        vb = Vc.bufs(blk - 5, hh * 128, hh * 128 + 128)
                    P.add("pe", lambda e, blk=blk, vap=vap: e.matmul(
                        ps[pbo][:, 0:128], lhsT=vap, rhs=pt.ap(blk), start=(blk == 0), stop=(blk == 6)),
                        reads=vb + pt.bufs(blk), writes=[psB[pbo]])
                for blk in range(7):
                    P.add("pe", lambda e, blk=blk: e.matmul(
                        ps[pbo][:, 128:256], lhsT=ones[:, :], rhs=pt.ap(blk), start=(blk == 0), stop=(blk == 6)),
                        reads=pt.bufs(blk) + [B_const], writes=[psB[pbo]])
                rd = st[:, half * 512:half * 512 + 128]
                P.add("dve", lambda e: e.reciprocal(out=rd, in_=ps[pbo][:, 128:256]),
                      reads=[psB[pbo]], writes=[B_st[half]])
                ot = OTst[hh]
                P.add("dve", lambda e: e.tensor_tensor(out=ot.ap(0, i * 128, i * 128 + 128),
                                                       in0=ps[pbo][:, 0:128], in1=rd, op=ALU.mult),
                      reads=[psB[pbo], B_st[half]], writes=ot.bufs(0, i * 128, i * 128 + 128))

            if not ATT:
                continue
            qk(0, 0)
            for i in range(NI):
                if i + 1 < NI:
                    qk(i + 1, (i + 1) % 2)
                rest(i, i % 2)
                if hh == 0 and i % 3 == 2:
                    ada_block()
            ot = OTst[hh]
            wb = [B_OTs[h][t] for t in range(4)]
            P.add("sp", lambda e, ot=ot, h=h: e.dma_start(out=OTs[h * 128:(h + 1) * 128, :], in_=ot.ap(0)),
                  reads=ot.bufs(0), writes=wb, dma=ot.bufs(0)[0])

    for hp in range(NPAIR):
        attention_pair(hp)
    while ada_next[0] < 48:
        ada_block()
    ada_finish()
    late_derived()
    if stage <= 2:
        return finish(nc, P)

    def load_A_from(src, Bsrc, arena, nk, tok0, ntok):
        v = View(arena, 0, nk, ntok)
        step = 4
        for k0 in range(0, nk, step):
            k1 = min(nk, k0 + step)
            bl = v.bufs3(k0, k1)
            rd = []
            for k in range(k0, k1):
                for t in range(tok0 // 512, (tok0 + ntok) // 512):
                    rd.append(Bsrc[k][t])
            srcap = src[k0 * 128:k1 * 128, tok0:tok0 + ntok].rearrange("(k p) t -> p k t", p=128)
            P.add("sp", lambda e, k0=k0, k1=k1, srcap=srcap: e.dma_start(out=v.ap3(k0, k1), in_=srcap),
                  reads=rd, writes=bl, dma=bl[0])
        return v

    def gemm_res_phase(Av, W, gs_name, res, Bres, dst, Bdst, pbs=(0, 1, 2, 3)):
        g = 0
        for blk in range(4):
            wv = wload(W[:, blk * 512:(blk + 1) * 512], 512, KC)
            for jj in range(4):
                fc = blk * 4 + jj
                for tt in range(4):
                    pb = pbs[g % len(pbs)]; g += 1
                    s = wslot()
                    P.add("sp", lambda e, s=s, fc=fc, tt=tt: e.dma_start(
                        out=wk[s][:, :], in_=res[fc * 128:(fc + 1) * 128, tt * 512:(tt + 1) * 512]),
                        reads=[Bres[fc][tt]] if Bres is not None else [], writes=[wkB[s]], dma=wkB[s])
                    gemm_group(pb, Av, wv, jj * 128, tt * 512, 512)
                    P.add("dve", lambda e, s=s, pb=pb, fc=fc: e.scalar_tensor_tensor(
                        out=wk[s][:, :], in0=ps[pb][:, :], scalar=der_ap(DER[gs_name], fc), in1=wk[s][:, :],
                        op0=ALU.mult, op1=ALU.add),
                        reads=[psB[pb], wkB[s], B_der], writes=[wkB[s]])
                    P.add("sp", lambda e, s=s, fc=fc, tt=tt: e.dma_start(
                        out=dst[fc * 128:(fc + 1) * 128, tt * 512:(tt + 1) * 512], in_=wk[s][:, :]),
                        reads=[wkB[s]], writes=[Bdst[fc][tt]], dma=wkB[s])

    def ln_phase(src, Bsrc, TW, eps, out_bf=None, bf_scale=None, bf_bias=None,
                 out_f32=None, Bout=None, f_scale=None, f_bias=None, scr_off=0, post=None,
                 pbs=(4, 5)):
        nt = T // TW
        nx = 2 * KC * TW
        sq = View(A, scr_off + 2 * nx, KC, TW)
        p1, p2 = pbs
        mean = st[:, 0:TW]
        rstd = st[:, 512:512 + TW]
        nmr = mean
        epsap = epsT[:, 0:1] if eps == EPS_RES else epsT[:, 1:2]

        def tile_views(ti):
            xoff = scr_off + (nx if ti % 2 else 0)
            xv = View(A, xoff, KC, 2 * TW)
            x32 = Af32[:, xoff // 2:xoff // 2 + KC * TW].rearrange("p (k t) -> p k t", t=TW)
            return xv, x32

        def stage_a(ti):
            t0 = ti * TW
            xv, x32 = tile_views(ti)
            rd = []
            for k in range(KC):
                for t in range(t0 // 512, (t0 + TW - 1) // 512 + 1):
                    rd.append(Bsrc[k][t])
            for k0 in range(0, KC, 4):
                gbufs = xv.bufs3(k0, k0 + 4)
                rdg = [Bsrc[k][t] for k in range(k0, k0 + 4) for t in range(t0 // 512, (t0 + TW - 1) // 512 + 1)]
                srcap = src[k0 * 128:(k0 + 4) * 128, t0:t0 + TW].rearrange("(k p) t -> p k t", p=128)
                P.add("sp", lambda e, k0=k0, srcap=srcap: e.dma_start(out=x32[:, k0:k0 + 4, :], in_=srcap),
                      reads=rdg, writes=gbufs, dma=gbufs[0])
            for k0 in range(0, KC, 4):
                P.add("act", lambda e, k0=k0: e.activation(out=sq.ap3(k0, k0 + 4), in_=x32[:, k0:k0 + 4, :],
                                                          func=AF.Square),
                      reads=xv.bufs3(k0, k0 + 4), writes=sq.bufs3(k0, k0 + 4))
            for k in range(KC):
                P.add("pe", lambda e, k=k: e.matmul(ps[p1][:, 0:TW], lhsT=onesf[:, :], rhs=x32[:, k, :],
                                                   start=(k == 0), stop=(k == KC - 1)),
                      reads=xv.bufs(k) + [B_const], writes=[psB[p1]])
            for k in range(KC):
                P.add("pe", lambda e, k=k: e.matmul(ps[p2][:, 0:TW], lhsT=ones[:, :], rhs=sq.ap(k),
                                                   start=(k == 0), stop=(k == KC - 1)),
                      reads=sq.bufs(k) + [B_const], writes=[psB[p2]])

        def stage_c(ti):
            P.add("dve", lambda e: e.tensor_scalar(out=mean, in0=ps[p1][:, 0:TW], scalar1=1.0 / D, scalar2=None,
                                                   op0=ALU.mult), reads=[psB[p1]], writes=[B_st[0]])
            P.add("dve", lambda e: e.tensor_tensor(out=rstd, in0=mean, in1=mean, op=ALU.mult),
                  reads=[B_st[0]], writes=[B_st[1]])
            P.add("dve", lambda e: e.scalar_tensor_tensor(out=rstd, in0=ps[p2][:, 0:TW], scalar=1.0 / D, in1=rstd,
                                                          op0=ALU.mult, op1=ALU.subtract),
                  reads=[psB[p2], B_st[1]], writes=[B_st[1]])
            P.add("act", lambda e: e.activation(out=rstd, in_=rstd, func=AF.Sqrt, bias=epsap, scale=1.0),
                  reads=[B_st[1], B_const], writes=[B_st[1]])
            P.add("dve", lambda e: e.reciprocal(out=rstd, in_=rstd), reads=[B_st[1]], writes=[B_st[1]])
            P.add("dve", lambda e: e.scalar_tensor_tensor(out=nmr, in0=mean, scalar=-1.0, in1=rstd,
                                                          op0=ALU.mult, op1=ALU.mult),
                  reads=[B_st[0], B_st[1]], writes=[B_st[0]])

        def stage_b(ti):
            t0 = ti * TW
            xv, x32 = tile_views(ti)
            G = 4
            for k0 in range(0, KC, G):
                xg = x32[:, k0:k0 + G, :]
                gb = xv.bufs3(k0, k0 + G)
                rb = rstd.unsqueeze(1).to_broadcast([128, G, TW])
                nb = nmr.unsqueeze(1).to_broadcast([128, G, TW])
                P.add("dve", lambda e, xg=xg, rb=rb: e.tensor_tensor(out=xg, in0=xg, in1=rb, op=ALU.mult),
                      reads=gb + [B_st[1]], writes=gb)
                P.add("dve", lambda e, xg=xg, nb=nb: e.tensor_tensor(out=xg, in0=xg, in1=nb, op=ALU.add),
                      reads=gb + [B_st[0]], writes=gb)
                for k in range(k0, k0 + G):
                    xk = x32[:, k, :]
                    kb = xv.bufs(k)
                    if out_f32 is not None:
                        if k % 4 != 3:
                            P.add("dve", lambda e, xk=xk, k=k: e.tensor_scalar(
                                out=xk, in0=xk, scalar1=f_scale(k), scalar2=f_bias(k), op0=ALU.mult, op1=ALU.add),
                                reads=kb + [B_small], writes=kb)
                        else:
                            P.add("act", lambda e, xk=xk, k=k: e.activation(
                                out=xk, in_=xk, func=AF.Identity, bias=f_bias(k), scale=f_scale(k)),
                                reads=kb + [B_small], writes=kb)
                    if out_bf is not None:
                        P.add("act", lambda e, xk=xk, k=k: e.activation(
                            out=out_bf.ap(k, t0, t0 + TW), in_=xk, func=AF.Identity,
                            bias=bf_bias(k), scale=bf_scale(k)),
                            reads=kb + [B_der, B_small, B_adaP], writes=out_bf.bufs(k, t0, t0 + TW))
                if out_f32 is not None:
                    wr = [Bout[k][t] for k in range(k0, k0 + G) for t in range(t0 // 512, (t0 + TW - 1) // 512 + 1)]
                    dstap = out_f32[k0 * 128:(k0 + G) * 128, t0:t0 + TW].rearrange("(k p) t -> p k t", p=128)
                    P.add("sp", lambda e, k0=k0, dstap=dstap: e.dma_start(out=dstap, in_=x32[:, k0:k0 + G, :]),
                          reads=gb, writes=wr, dma=gb[0])

        stage_a(0)
        stage_c(0)
        for ti in range(nt):
            if ti + 1 < nt:
                stage_a(ti + 1)
            stage_b(ti)
            if ti + 1 < nt:
                stage_c(ti + 1)
            if post is not None:
                post(ti, ti * TW)

    Af32 = A.t[:, :].bitcast(F32)

    def ffn(l, hT):
        Win = ffn_w_in[l]
        Wout = ffn_w_out[l]
        g = [0]

        def ffn_in_group(wg, wu, cc, c, tt):
            pg, pu = ((0, 1), (2, 3))[g[0] % 2]; g[0] += 1
            gemm_group(pg, hT, wg, cc * 128, tt * 512, 512)
            gemm_group(pu, hT, wu, cc * 128, tt * 512, 512)
            s = wslot()
            P.add("act", lambda e: e.activation(out=wk[s][:, :], in_=ps[pg][:, :], func=AF.Silu),
                  reads=[psB[pg]], writes=[wkB[s]])
            hs = hslot()
            P.add("dve", lambda e: e.tensor_tensor(out=hb[hs][:, :], in0=ps[pu][:, :], in1=wk[s][:, :], op=ALU.mult),
                  reads=[psB[pu], wkB[s]], writes=[hbB[hs]])
            P.add("sp", lambda e: e.dma_start(out=Gs[c * 128:(c + 1) * 128, tt * 512:(tt + 1) * 512], in_=hb[hs][:, :]),
                  reads=[hbB[hs]], writes=[B_Gs[c][tt]], dma=hbB[hs])

        def ffn_w(hbk):
            return (wload(Win[:, hbk * 256:(hbk + 1) * 256], 256, KC),
                    wload(Win[:, DFF + hbk * 256:DFF + (hbk + 1) * 256], 256, KC))

        w0 = ffn_w(0)
        w1 = ffn_w(1)
        for tt in range(4):
            for hbk, (wg, wu) in ((0, w0), (1, w1)):
                for cc in range(2):
                    ffn_in_group(wg, wu, cc, hbk * 2 + cc, tt)
        for hbk in range(2, FC // 2):
            wg, wu = ffn_w(hbk)
            for cc in range(2):
                for tt in range(4):
                    ffn_in_group(wg, wu, cc, hbk * 2 + cc, tt)
        gname = f"g2_{l}"
        for th in range(2):
            gv = load_A_from(Gs, B_Gs, A, FC, th * 1024, 1024)
            for fb in range(8):
                wA = wload(Wout[0:22 * 128, fb * 256:(fb + 1) * 256], 256, 22)
                wB = wload(Wout[22 * 128:44 * 128, fb * 256:(fb + 1) * 256], 256, 22)
                pset = (0, 1, 2, 3) if fb % 2 == 0 else (4, 5, 6, 7)
                grp = [(jj, t2) for jj in range(2) for t2 in range(2)]
                slots = []
                for gi, (jj, t2) in enumerate(grp):
                    fc = fb * 2 + jj
                    tt = th * 2 + t2
                    s = wslot()
                    slots.append(s)
                    P.add("sp", lambda e, s=s, fc=fc, tt=tt: e.dma_start(
                        out=wk[s][:, :], in_=Hres[fc * 128:(fc + 1) * 128, tt * 512:(tt + 1) * 512]),
                        reads=[B_Hres[fc][tt]], writes=[wkB[s]], dma=wkB[s])
                for gi, (jj, t2) in enumerate(grp):
                    gemm_group(pset[gi], gv, wA, jj * 128, t2 * 512, 512, koff=0, start=True, stop=False)
                for gi, (jj, t2) in enumerate(grp):
                    gemm_group(pset[gi], gv, wB, jj * 128, t2 * 512, 512, koff=22, start=False, stop=True)
                for gi, (jj, t2) in enumerate(grp):
                    fc = fb * 2 + jj
                    tt = th * 2 + t2
                    s = slots[gi]
                    pb = pset[gi]
                    P.add("dve", lambda e, s=s, pb=pb, fc=fc: e.scalar_tensor_tensor(
                        out=wk[s][:, :], in0=ps[pb][:, :], scalar=der_ap(DER[gname], fc), in1=wk[s][:, :],
                        op0=ALU.mult, op1=ALU.add),
                        reads=[psB[pb], wkB[s], B_der], writes=[wkB[s]])
                    P.add("sp", lambda e, s=s, fc=fc, tt=tt: e.dma_start(
                        out=Hpre[fc * 128:(fc + 1) * 128, tt * 512:(tt + 1) * 512], in_=wk[s][:, :]),
                        reads=[wkB[s]], writes=[B_Hpre[fc][tt]], dma=wkB[s])

    def lnp_col(l, wh, gb):
        base = ((l * 2 + wh) * 2 + gb) * KC
        return lambda k: lnps[:, base + k:base + k + 1]

    def der_col(name):
        return lambda k: der_ap(DER[name], k)

    Bx = None
    OTv = load_A_from(OTs, B_OTs, A, KC, 0, T)
    gemm_res_phase(OTv, w_o[:, :], "g1_0", xT, None, Hpre, B_Hpre)
    if stage <= 3:
        return finish(nc, P)
    hT = View(Bq, 0, KC, T)
    ln_phase(Hpre, B_Hpre, 512, EPS_RES, out_bf=hT, bf_scale=der_col("S2_0"), bf_bias=(lambda k: adaP[:, (0 * 96 + 3 * KC + k) * 2:(0 * 96 + 3 * KC + k) * 2 + 1]),
             out_f32=Hres, Bout=B_Hres, f_scale=lnp_col(0, 0, 0), f_bias=lnp_col(0, 0, 1))
    if stage <= 4:
        return finish(nc, P)
    ffn(0, hT)
    if stage <= 5:
        return finish(nc, P)
    ln_phase(Hpre, B_Hpre, 512, EPS_RES, out_bf=hT, bf_scale=der_col("S1_1"), bf_bias=(lambda k: adaP[:, (1 * 96 + 0 * KC + k) * 2:(1 * 96 + 0 * KC + k) * 2 + 1]),
             out_f32=Hres, Bout=B_Hres, f_scale=lnp_col(0, 1, 0), f_bias=lnp_col(0, 1, 1))
    if stage <= 6:
        return finish(nc, P)

    uT = View(A, 0, KC, T)
    g = [0]

    def gm_group(wv, jj, fc, tt):
        pb = (0, 1, 2, 3)[g[0] % 4]; g[0] += 1
        gemm_group(pb, hT, wv, jj * 128, tt * 512, 512)
        if fc < KC:
            P.add("act", lambda e: e.activation(out=uT.ap(fc, tt * 512, tt * 512 + 512), in_=ps[pb][:, :],
                                                func=AF.Gelu),
                  reads=[psB[pb]], writes=uT.bufs(fc, tt * 512, tt * 512 + 512))
        else:
            s = wslot()
            P.add("act", lambda e: e.activation(out=wk[s][:, :], in_=ps[pb][:, :], func=AF.Gelu),
                  reads=[psB[pb]], writes=[wkB[s]])
            P.add("sp", lambda e: e.dma_start(
                out=Vs[(fc - KC) * 128:(fc - KC + 1) * 128, tt * 512:(tt + 1) * 512], in_=wk[s][:, :]),
                reads=[wkB[s]], writes=[B_Vs[fc - KC][tt]], dma=wkB[s])

    gw = [wload(gm_w_in[:, blk * 512:(blk + 1) * 512], 512, KC) for blk in range(2)]
    for tt in range(4):
        for blk in range(2):
            for jj in range(4):
                gm_group(gw[blk], jj, blk * 4 + jj, tt)
    for blk in range(2, 8):
        wv = wload(gm_w_in[:, blk * 512:(blk + 1) * 512], 512, KC)
        for jj in range(4):
            for tt in range(4):
                gm_group(wv, jj, blk * 4 + jj, tt)
    if stage <= 7:
        return finish(nc, P)

    SCR = 32768
    vn = View(Wa, 4096, KC, 128)
    vT = View(Wa, 12288, KC, 128)
    wsv = View(Wa, 0, KC, 128)
    P.add("pool", lambda e: e.dma_start(out=Wa.t[:, 0:2048], in_=w_sT), writes=[Wa.bufs[0]], dma=Wa.bufs[0])
    Wf32 = Wa.t[:, :].bitcast(F32)
    bsv = Wf32[:, 4096:4096 + 2048]
    P.add("pool", lambda e: e.dma_start(out=bsv, in_=bsb), writes=Wa.bufs[2:3], dma=Wa.bufs[2])
    psT = [ps[6][:, :].bitcast(BF16), ps[7][:, :].bitcast(BF16)]
    zT = View(Bq, 0, KC, T)

    def spatial(ti, t0):
        for half in range(2):
            for gi in range(8):
                gg = half * 8 + gi
                P.add("pe", lambda e, gg=gg, gi=gi, half=half: e.transpose(
                    out=psT[half][:, gi * 128:(gi + 1) * 128], in_=vn.ap(gg), identity=ident[:, :]),
                    reads=vn.bufs(gg) + [B_const], writes=[psB[6 + half]])
            P.add("dve" if half == 0 else "act",
                  (lambda e, half=half: e.tensor_copy(out=vT.ap3(half * 8, half * 8 + 8), in_=psT[half].rearrange(
                      "p (g c) -> p g c", c=128))) if half == 0 else
                  (lambda e, half=half: e.activation(out=vT.ap3(half * 8, half * 8 + 8), in_=psT[half].rearrange(
                      "p (g c) -> p g c", c=128), func=AF.Copy)),
                  reads=[psB[6 + half]], writes=vT.bufs3(half * 8, half * 8 + 8))
        for q4 in range(4):
            pb = q4
            for gi in range(4):
                gg = q4 * 4 + gi
                P.add("pe", lambda e, gg=gg, gi=gi, pb=pb: e.matmul(
                    ps[pb][:, gi * 128:(gi + 1) * 128], lhsT=vT.ap(gg), rhs=wsv.ap(gg), start=True, stop=True),
                    reads=vT.bufs(gg) + [Wa.bufs[0]], writes=[psB[pb]])
            s = wslot()
            P.add("dve", lambda e, s=s, pb=pb, q4=q4: e.tensor_tensor(
                out=wk[s][:, :], in0=ps[pb][:, :], in1=bsv[:, q4 * 512:(q4 + 1) * 512], op=ALU.add),
                reads=[psB[pb]] + Wa.bufs[2:3], writes=[wkB[s]])
            P.add("pool", lambda e, s=s, q4=q4, t0=t0: e.tensor_tensor(
                out=zT.ap3(q4 * 4, q4 * 4 + 4, t0, t0 + 128),
                in0=wk[s][:, :].rearrange("p (g c) -> p g c", c=128),
                in1=uT.ap3(q4 * 4, q4 * 4 + 4, t0, t0 + 128), op=ALU.mult),
                reads=[wkB[s]] + uT.bufs3(q4 * 4, q4 * 4 + 4, t0, t0 + 128),
                writes=zT.bufs3(q4 * 4, q4 * 4 + 4, t0, t0 + 128))

    ln_phase(Vs, B_Vs, 128, LN_EPS, out_bf=_VNView(vn), bf_scale=lambda k: gmlns[:, k:k + 1],
             bf_bias=lambda k: gmlns[:, KC + k:KC + k + 1], scr_off=SCR, post=spatial, pbs=(4, 5))
    if stage <= 8:
        return finish(nc, P)
    gemm_res_phase(zT, gm_w_out[:, :], "g1_1", Hres, B_Hres, Hpre, B_Hpre)
    ln_phase(Hpre, B_Hpre, 512, EPS_RES, out_bf=hT, bf_scale=der_col("S2_1"), bf_bias=(lambda k: adaP[:, (1 * 96 + 3 * KC + k) * 2:(1 * 96 + 3 * KC + k) * 2 + 1]),
             out_f32=Hres, Bout=B_Hres, f_scale=lnp_col(1, 0, 0), f_bias=lnp_col(1, 0, 1))
    if stage <= 9:
        return finish(nc, P)
    ffn(1, hT)
    ln_phase(Hpre, B_Hpre, 512, EPS_RES, out_f32=outT, Bout=B_out, f_scale=lnp_col(1, 1, 0),
             f_bias=lnp_col(1, 1, 1))
    return finish(nc, P)


class _VNView:
    def __init__(self, vn):
        self.vn = vn

    def ap(self, k, lo, hi):
        return self.vn.ap(k, 0, hi - lo)

    def bufs(self, k, lo, hi):
        return self.vn.bufs(k, 0, hi - lo)


LASTP = None


def finish(nc, P):
    global LASTP
    LASTP = P
    with nc.Block() as block:
        P.emit(block)
    return nc


def _ext_rows(half):
    return list(range(36)) if half == 0 else list(range(63, 27, -1))


def _bias_tables(rpb, half):
    rows = np.asarray(_ext_rows(half))
    out = np.full((NH, 128, NPAT, 5, 128), MASKV, np.float32)
    cq = np.arange(64)
    c0 = np.clip(cq - 8, 0, 48)
    for pat in range(NPAT):
        i = pat
        kb0 = max(i - 2, 0)
        for lq in range(2):
            gq = rows[2 * i + lq]
            r0 = int(np.clip(gq - 4, 0, 56))
            for blk in range(5):
                for lk in range(2):
                    gk = rows[2 * (kb0 + blk) + lk]
                    if not (r0 <= gk < r0 + 8):
                        continue
                    dr = gk - gq + 7
                    ck = np.arange(64)
                    valid = (ck[:, None] >= c0[None, :]) & (ck[:, None] < c0[None, :] + 16)
                    dc = ck[:, None] - cq[None, :] + 15
                    vals = rpb[:, dr, :][:, np.clip(dc, 0, 30)]
                    blkv = np.where(valid[None], vals, MASKV)
                    out[:, lk * 64:(lk + 1) * 64, pat, blk, lq * 64:(lq + 1) * 64] = blkv
    return np.ascontiguousarray(out.reshape(NH, 128, NPAT * 5 * 128))


def _pp(v):
    v = np.asarray(v, np.float32)
    lead = v.shape[:-1]
    n = v.shape[-1] // 128
    a = v.reshape(lead + (n, 128))
    return np.ascontiguousarray(np.moveaxis(a, -1, 0))


def make_in_maps(x, c, ctx, c_ctx, ada_w, ada_b, ln_g, ln_b, na_w_qkv, na_w_o, na_rpb,
                 gm_w_in, gm_ln_g, gm_ln_b, gm_w_s, gm_b_s, gm_w_out, ffn_w_in, ffn_w_out, cores=range(8)):
    f = lambda a: np.ascontiguousarray(np.asarray(a, dtype=np.float32))
    x = f(x); ctx = f(ctx)
    shared = dict(
        ada_w=f(ada_w), w_qkv=f(na_w_qkv[0]), w_o=f(na_w_o[0]), gm_w_in=f(gm_w_in[0]),
        gm_w_out=f(gm_w_out[0]), ffn_w_in=f(ffn_w_in), ffn_w_out=f(ffn_w_out),
    )
    shared["identin"] = np.eye(128, dtype=np.float32)
    ab = _pp(f(ada_b))
    shared["adab"] = np.ascontiguousarray(np.repeat(ab[:, :, :, None], 2, axis=3).reshape(128, -1))
    lg = _pp(f(ln_g)); lb = _pp(f(ln_b))
    shared["lnp"] = np.ascontiguousarray(np.stack([lg, lb], axis=3).reshape(128, -1))
    shared["gmln"] = np.ascontiguousarray(
        np.stack([_pp(f(gm_ln_g[0])), _pp(f(gm_ln_b[0]))], axis=1).reshape(128, -1))
    rpb = f(na_rpb[0])
    bias_half = [_bias_tables(rpb, 0), _bias_tables(rpb, 1)]
    ws = f(gm_w_s[0])
    bs = f(gm_b_s[0])
    perm = (np.arange(128) + 64) % 128
    wsT = [None, None]
    bsbb = [None, None]
    for half in range(2):
        w = ws if half == 0 else ws[:, perm][:, :, perm]
        b = bs if half == 0 else bs[:, perm]
        wsT[half] = np.ascontiguousarray(np.transpose(w, (2, 0, 1)).reshape(128, -1))
        bsbb[half] = np.ascontiguousarray(np.broadcast_to(b.reshape(1, -1), (128, KC * 128)))
    maps = []
    for core in cores:
        b, half = core // 2, core % 2
        rows = _ext_rows(half)
        xe = x[b].reshape(64, 64, D)[rows].reshape(TE, D)
        m = dict(shared)
        m["xT"] = np.ascontiguousarray(xe.T)
        m["ctxT"] = np.ascontiguousarray(ctx[b].T)
        cvv = np.stack([_pp(f(c[b])), _pp(f(c_ctx))], axis=2)
        m["cvec"] = np.ascontiguousarray(cvv.reshape(128, -1))
        m["biasT"] = bias_half[half]
        m["w_sT"] = wsT[half]
        m["bsb"] = bsbb[half]
        maps.append(m)
    return maps


def assemble(outs, cores=range(8)):
    out = np.zeros((4, 64, 64, D), np.float32)
    for core, oT in zip(cores, outs):
        b, half = core // 2, core % 2
        rows = _ext_rows(half)[:32]
        out[b, rows] = np.ascontiguousarray(oT.T).reshape(32, 64, D)
    return out.reshape(4, 4096, D)


_NC = None


def kernel(**inputs):
    global _NC
    if _NC is None:
        _NC = build()
    in_maps = make_in_maps(**inputs)
    res = run_bass_kernel_spmd(_NC, in_maps, core_ids=list(range(8)))
    return assemble([r["outT"] for r in res.results])
```

```python
import numpy as np
import concourse.bass as bass
import concourse.mybir as mybir
from concourse.bass_utils import run_bass_kernel_spmd

F32 = mybir.dt.float32
BF16 = mybir.dt.bfloat16
AF = mybir.ActivationFunctionType
ALU = mybir.AluOpType

D = 2048
KC = 16
T = 2048
TE = 2304
NCTX = 256
NH = 16
DFF = 5632
FC = 44
DEPTH = 2
ALPHA = (2 * DEPTH) ** 0.25
LN_EPS = 1e-5
EPS_RES = LN_EPS / (ALPHA * ALPHA)
MASKV = -30000.0
NPAT = 3

ENGS = ("sp", "act", "pe", "dve", "pool")
DECL = []


class Buf:
    __slots__ = ("name", "last_w", "reads", "dsem", "ndma")

    def __init__(self, name):
        self.name = name
        self.last_w = None
        self.reads = []
        self.dsem = None
        self.ndma = 0


class Op:
    __slots__ = ("eng", "fn", "deps", "dma", "sig", "val", "sem")

    def __init__(self, eng, fn):
        self.eng = eng
        self.fn = fn
        self.deps = []
        self.dma = None
        self.sig = False
        self.val = 0
        self.sem = None


class Prog:
    def __init__(self, nc):
        self.nc = nc
        self.ops = {e: [] for e in ENGS}
        self.nsem = 0

    def newsem(self, name):
        self.nsem += 1
        return self.nc.alloc_semaphore(name)

    def add(self, eng, fn, reads=(), writes=(), dma=None):
        op = Op(eng, fn)
        deps = []
        for b in reads:
            if b.last_w is not None:
                deps.append(b.last_w)
        for b in writes:
            if b.last_w is not None:
                deps.append(b.last_w)
            deps.extend(b.reads)
        seen = set()
        for d in deps:
            if id(d) in seen or d is op:
                continue
            seen.add(id(d))
            if d.eng == "pe" and eng == "pe" and d.dma is None:
                continue
            op.deps.append(d)
            d.sig = True
        for b in reads:
            b.reads.append(op)
        for b in writes:
            b.last_w = op
            b.reads = []
        if dma is not None:
            op.dma = dma
            if dma.dsem is None:
                dma.dsem = self.newsem("d_" + dma.name)
            dma.ndma += 1
            op.sem = dma.dsem
            op.val = 16 * dma.ndma
        self.ops[eng].append(op)
        return op

    def emit(self, block):
        for e in ENGS:
            cnt = 0
            sem = None
            for op in self.ops[e]:
                if op.dma is not None:
                    continue
                if op.sig:
                    if sem is None:
                        sem = self.newsem("e_" + e)
                    cnt += 1
                    op.sem = sem
                    op.val = cnt
        P = self

        def run(e, eng):
            waited = {}
            for op in P.ops[e]:
                for d in op.deps:
                    key = d.sem.num
                    if waited.get(key, 0) >= d.val:
                        continue
                    eng.wait_ge(d.sem, d.val)
                    waited[key] = d.val
                ins = op.fn(eng)
                if op.dma is not None:
                    ins.then_inc(op.sem, 16)
                elif op.sig:
                    ins.then_inc(op.sem, 1)
            fin = {}
            for op in P.ops[e]:
                if op.dma is not None:
                    prev = fin.get(op.sem.num, (None, 0))[1]
                    fin[op.sem.num] = (op.sem, max(prev, op.val))
            for sem, v in fin.values():
                if waited.get(sem.num, 0) < v:
                    eng.wait_ge(sem, v)

        block.sync(lambda eng: run("sp", eng))
        block.scalar(lambda eng: run("act", eng))
        block.tensor(lambda eng: run("pe", eng))
        block.vector(lambda eng: run("dve", eng))
        block.gpsimd(lambda eng: run("pool", eng))


class Arena:
    def __init__(self, nc, name, nelem, dtype, unit):
        self.t = nc.alloc_sbuf_tensor(name, [128, nelem], dtype)
        self.unit = unit
        self.bufs = [Buf(f"{name}{i}") for i in range((nelem + unit - 1) // unit)]
        self.nelem = nelem

    def u(self, lo, hi):
        return self.bufs[lo // self.unit:(hi - 1) // self.unit + 1]


class View:
    def __init__(self, arena, off, a, b):
        self.ar = arena
        self.off = off
        self.a = a
        self.b = b
        assert off + a * b <= arena.nelem, (off, a, b, arena.nelem)

    def ap(self, i, lo=0, hi=None):
        hi = self.b if hi is None else hi
        s = self.off + i * self.b
        return self.ar.t[:, s + lo:s + hi]

    def ap3(self, i0, i1, lo=0, hi=None):
        hi = self.b if hi is None else hi
        full = self.ar.t[:, self.off:self.off + self.a * self.b].rearrange("p (a b) -> p a b", b=self.b)
        return full[:, i0:i1, lo:hi]

    def bufs(self, i, lo=0, hi=None):
        hi = self.b if hi is None else hi
        s = self.off + i * self.b
        return self.ar.u(s + lo, s + hi)

    def bufs3(self, i0, i1, lo=0, hi=None):
        out = []
        for i in range(i0, i1):
            for b in self.bufs(i, lo, hi):
                if not out or out[-1] is not b:
                    out.append(b)
        return out


def build(stage=99, debug=False):
    nc = bass.Bass("TRN2", target_bir_lowering=False)
    P = Prog(nc)

    import os
    SKIP_ADA = debug and os.environ.get("K_SKIPADA") == "1"
    NPAIR = int(os.environ.get("K_NPAIR", "8")) if debug else 8
    ATT = int(os.environ.get("K_ATT", "1")) if debug else 1
    NI = int(os.environ.get("K_NI", "16")) if debug else 16
    DECL.clear()

    def din(name, shape, dt=F32, need=True):
        if not need:
            return None
        DECL.append(name)
        return nc.dram_tensor(name, list(shape), dt, kind="ExternalInput").ap()

    def dscr(name, shape, dt):
        kind = "ExternalOutput" if debug else "Internal"
        return nc.dram_tensor(name, list(shape), dt, kind=kind).ap()

    xT = din("xT", [D, TE])
    ctxT = din("ctxT", [D, NCTX])
    cvec = din("cvec", [128, KC * 2])
    ada_w = din("ada_w", [DEPTH, D, 6 * D], need=not SKIP_ADA)
    adab = din("adab", [128, DEPTH * 96 * 2])
    lnp = din("lnp", [128, DEPTH * 2 * 2 * KC])
    w_qkv = din("w_qkv", [D, 3 * D])
    w_o = din("w_o", [D, D])
    biasT = din("biasT", [NH, 128, NPAT * 5 * 128])
    gm_w_in = din("gm_w_in", [D, 2 * D], need=stage >= 7)
    gmln = din("gmln", [128, 2 * KC])
    w_sT = din("w_sT", [128, KC * 128])
    bsb = din("bsb", [128, KC * 128])
    identin = din("identin", [128, 128])
    gm_w_out = din("gm_w_out", [D, D], need=stage >= 9)
    ffn_w_in = din("ffn_w_in", [DEPTH, D, 2 * DFF], need=stage >= 5)
    ffn_w_out = din("ffn_w_out", [DEPTH, DFF, D], need=stage >= 5)
    outT = nc.dram_tensor("outT", [D, T], F32, kind="ExternalOutput").ap()

    OTs = dscr("OTs", [D, T], BF16)
    Hpre = dscr("Hpre", [D, T], F32)
    Hres = dscr("Hres", [D, T], F32)
    Gs = dscr("Gs", [DFF, T], BF16)
    Vs = dscr("Vs", [D, T], F32)

    def dgrid(name, nch):
        return [[Buf(f"{name}_{c}_{t}") for t in range(4)] for c in range(nch)]

    B_OTs = dgrid("OTs", KC)
    B_Hpre = dgrid("Hpre", KC)
    B_Hres = dgrid("Hres", KC)
    B_Gs = dgrid("Gs", FC)
    B_Vs = dgrid("Vs", KC)
    B_out = dgrid("out", KC)

    A = Arena(nc, "arA", 45056, BF16, 512)
    Bq = Arena(nc, "arB", 32768, BF16, 512)
    Wa = Arena(nc, "arW", 16384, BF16, 4096)
    NWK = 5
    wk = [nc.alloc_sbuf_tensor(f"wk{i}", [128, 512], F32) for i in range(NWK)]
    wkB = [Buf(f"wk{i}") for i in range(NWK)]
    NHB = 3
    hb = [nc.alloc_sbuf_tensor(f"hb{i}", [128, 512], BF16) for i in range(NHB)]
    hbB = [Buf(f"hb{i}") for i in range(NHB)]
    ident = nc.alloc_sbuf_tensor("ident", [128, 128], BF16)
    ones = nc.alloc_sbuf_tensor("ones", [128, 128], BF16)
    onesf = nc.alloc_sbuf_tensor("onesf", [128, 128], F32)
    i2f = nc.alloc_sbuf_tensor("i2f", [128, 2], F32)
    B_const = Buf("const")
    adaP = nc.alloc_sbuf_tensor("adaP", [128, DEPTH * 96 * 2], F32)
    B_adaP = Buf("adaP")
    adabs = nc.alloc_sbuf_tensor("adabs", [128, DEPTH * 96 * 2], F32)
    B_adabs = Buf("adabs")
    lnps = nc.alloc_sbuf_tensor("lnps", [128, DEPTH * 2 * 2 * KC], F32)
    gmlns = nc.alloc_sbuf_tensor("gmlns", [128, 2 * KC], F32)
    cv = nc.alloc_sbuf_tensor("cv", [128, KC * 2], F32)
    scb = nc.alloc_sbuf_tensor("scb", [128, KC * 2], BF16)
    B_small = Buf("small")
    NDER = 20
    der = nc.alloc_sbuf_tensor("der", [128, NDER * KC], F32)
    B_der = Buf("der")
    epsT = nc.alloc_sbuf_tensor("epsT", [128, 2], F32)
    st = nc.alloc_sbuf_tensor("st", [128, 2 * 512], F32)
    B_st = [Buf(f"st{i}") for i in range(2)]

    ps = [nc.alloc_psum_tensor(f"ps{i}", [128, 512], F32) for i in range(8)]
    psB = [Buf(f"ps{i}") for i in range(8)]

    wrr = [0]

    def wslot():
        s = wrr[0] % NWK
        wrr[0] += 1
        return s

    hrr = [0]

    def hslot():
        s = hrr[0] % NHB
        hrr[0] += 1
        return s

    P.add("pool", lambda e: e.dma_start(out=ident[:], in_=identin), writes=[B_const], dma=B_const)
    P.add("dve", lambda e: e.memset(ones[:], 1.0), writes=[B_const])
    P.add("dve", lambda e: e.memset(onesf[:], 1.0), writes=[B_const])
    P.add("act", lambda e: e.activation(out=i2f[0:2, 0:2], in_=ident[0:2, 0:2], func=AF.Copy),
          reads=[B_const], writes=[B_const])
    P.add("dve", lambda e: e.memset(der[:, :], 0.0), writes=[B_der])
    P.add("dve", lambda e: e.memset(epsT[:, 0:1], EPS_RES), writes=[B_const])
    P.add("dve", lambda e: e.memset(epsT[:, 1:2], LN_EPS), writes=[B_const])
    P.add("sp", lambda e: e.dma_start(out=cv[:], in_=cvec), writes=[B_small], dma=B_small)
    P.add("sp", lambda e: e.dma_start(out=lnps[:], in_=lnp), writes=[B_small], dma=B_small)
    P.add("sp", lambda e: e.dma_start(out=gmlns[:], in_=gmln), writes=[B_small], dma=B_small)
    P.add("sp", lambda e: e.dma_start(out=adabs[:, :], in_=adab), writes=[B_adabs], dma=B_adabs)
    P.add("act", lambda e: e.activation(out=scb[:], in_=cv[:], func=AF.Silu), reads=[B_small], writes=[B_small])

    wq = [0]

    def wload(src_ap, ncols, nk, unit=None):
        nelem = nk * ncols
        nunits = (nelem + 4095) // 4096
        assert nunits in (1, 2, 4)
        if unit is None:
            u0 = (wq[0] % (4 // nunits)) * nunits
            wq[0] += 1
        else:
            u0 = unit
        v = View(Wa, u0 * 4096, nk, ncols)
        bl = Wa.bufs[u0:u0 + nunits]
        dst = Wa.t[:, u0 * 4096:u0 * 4096 + nelem].rearrange("p (k n) -> p k n", n=ncols)
        src = src_ap.rearrange("(k p) n -> p k n", p=128)
        P.add("pool", lambda e: e.dma_start(out=dst, in_=src), writes=bl, dma=bl[0])
        return v

    def der_ap(i, j):
        return der[:, i * KC + j:i * KC + j + 1]

    def der_row(i):
        return der[:, i * KC:(i + 1) * KC]

    def ada_ap(l, g, which):
        v = adaP[:, :].rearrange("p (l j w) -> p l j w", l=DEPTH, w=2)
        return v[:, l, g * KC:(g + 1) * KC, which]

    def lnp_ap(l, wh, gb):
        v = lnps[:, :].rearrange("p (l a g k) -> p l a g k", l=DEPTH, a=2, g=2)
        return v[:, l, wh, gb, :]

    ada_next = [0]
    ada_pend = [None]

    NADA = 96

    def ada_finish():
        pend = ada_pend[0]
        ada_pend[0] = None
        if pend is None:
            return
        l, hb_, s, psb = pend
        for jj in range(2):
            P.add("pe", lambda e, jj=jj: e.matmul(ps[psb][:, jj * 2:jj * 2 + 2],
                                                  lhsT=wk[s][0:2, jj * 128:(jj + 1) * 128], rhs=i2f[0:2, 0:2],
                                                  start=True, stop=True),
                  reads=[wkB[s], B_const], writes=[psB[psb]])
        o = l * 192 + hb_ * 4
        P.add("dve", lambda e: e.tensor_tensor(out=adaP[:, o:o + 4], in0=ps[psb][:, 0:4],
                                               in1=adabs[:, o:o + 4], op=ALU.add),
              reads=[psB[psb], B_adabs], writes=[B_adaP])

    def ada_block(unit=3):
        gb = ada_next[0]
        ada_next[0] += 1
        if gb >= NADA or SKIP_ADA:
            return
        l, hb_ = gb // 48, gb % 48
        psb = 6 + gb % 2
        v = wload(ada_w[l, :, hb_ * 256:(hb_ + 1) * 256], 256, KC, unit=unit)
        for k in range(KC):
            P.add("pe", lambda e, k=k: e.matmul(ps[psb][0:2, 0:256], lhsT=scb[:, k * 2:k * 2 + 2], rhs=v.ap(k),
                                               start=(k == 0), stop=(k == KC - 1)),
                  reads=v.bufs(k) + [B_small], writes=[psB[psb]])
        s = wslot()
        P.add("dve", lambda e: e.tensor_copy(out=wk[s][0:2, 0:256], in_=ps[psb][0:2, 0:256]),
              reads=[psB[psb]], writes=[wkB[s]])
        ada_finish()
        ada_pend[0] = (l, hb_, s, psb)

    if SKIP_ADA:
        P.add("dve", lambda e: e.memset(adaP[:, :], 0.25), writes=[B_adaP])
    for j in range(16):
        ada_block(unit=j % 4)
    ada_finish()

    DER = {}

    def dalloc(name):
        DER[name] = len(DER)
        assert len(DER) <= NDER
        return DER[name]

    def dve_der(fn):
        P.add("dve", fn, reads=[B_adaP, B_small, B_der], writes=[B_der])

    i = dalloc("S1_0")
    dve_der(lambda e, i=i: e.tensor_scalar(out=der_row(i), in0=ada_ap(0, 1, 0), scalar1=1.0, scalar2=None,
                                           op0=ALU.add))
    i = dalloc("B1_0")
    dve_der(lambda e, i=i: e.tensor_copy(out=der_row(i), in_=ada_ap(0, 0, 0)))
    i = dalloc("cS1")
    dve_der(lambda e, i=i: e.tensor_scalar(out=der_row(i), in0=ada_ap(0, 1, 1), scalar1=1.0, scalar2=None,
                                           op0=ALU.add))
    i = dalloc("cB1")
    dve_der(lambda e, i=i: e.tensor_copy(out=der_row(i), in_=ada_ap(0, 0, 1)))

    def late_derived():
        for l in range(DEPTH):
            if l > 0:
                i = dalloc(f"S1_{l}")
                dve_der(lambda e, i=i, l=l: e.tensor_scalar(out=der_row(i), in0=ada_ap(l, 1, 0), scalar1=1.0,
                                                            scalar2=None, op0=ALU.add))
            i = dalloc(f"g1_{l}")
            dve_der(lambda e, i=i, l=l: e.tensor_scalar(out=der_row(i), in0=ada_ap(l, 2, 0), scalar1=1.0 / ALPHA,
                                                        scalar2=None, op0=ALU.mult))
            i = dalloc(f"g2_{l}")
            dve_der(lambda e, i=i, l=l: e.tensor_scalar(out=der_row(i), in0=ada_ap(l, 5, 0), scalar1=1.0 / ALPHA,
                                                        scalar2=None, op0=ALU.mult))
            i = dalloc(f"S2_{l}")
            dve_der(lambda e, i=i, l=l: e.tensor_scalar(out=der_row(i), in0=ada_ap(l, 4, 0), scalar1=1.0,
                                                        scalar2=None, op0=ALU.add))

        def mod_after_ln(name, l_ln, wh, sname, shift_l, shift_g):
            ia = dalloc(name + "_a")
            ib = dalloc(name + "_b")
            dve_der(lambda e: e.tensor_tensor(out=der_row(ia), in0=lnp_ap(l_ln, wh, 0), in1=der_row(DER[sname]),
                                              op=ALU.mult))
            dve_der(lambda e: e.tensor_tensor(out=der_row(ib), in0=lnp_ap(l_ln, wh, 1), in1=der_row(DER[sname]),
                                              op=ALU.mult))
            dve_der(lambda e: e.tensor_tensor(out=der_row(ib), in0=der_row(ib), in1=ada_ap(shift_l, shift_g, 0),
                                              op=ALU.add))

        mod_after_ln("m00", 0, 0, "S2_0", 0, 3)
        mod_after_ln("m01", 0, 1, "S1_1", 1, 0)
        mod_after_ln("m10", 1, 0, "S2_1", 1, 3)

    if stage <= 0:
        return finish(nc, P)

    gemm_rr = [0]

    def gemm_group(pb, Aview, wv, col0, tok0, ntok, kchunks=None, koff=0, start=True, stop=True):
        nk = wv.a if kchunks is None else kchunks
        for k in range(nk):
            P.add("pe", lambda e, k=k: e.matmul(ps[pb][:, 0:ntok], lhsT=wv.ap(k, col0, col0 + 128),
                                                rhs=Aview.ap(k + koff, tok0, tok0 + ntok),
                                                start=(start and k == 0), stop=(stop and k == nk - 1)),
                  reads=wv.bufs(k, col0, col0 + 128) + Aview.bufs(k + koff, tok0, tok0 + ntok),
                  writes=[psB[pb]])

    aT = View(A, 0, KC, TE)
    acT = View(Bq, 0, KC, NCTX)
    for k in range(KC):
        for t0 in range(0, TE, 512):
            n = min(512, TE - t0)
            s = wslot()
            P.add("sp", lambda e, s=s, k=k, t0=t0, n=n: e.dma_start(out=wk[s][:, 0:n],
                                                                  in_=xT[k * 128:(k + 1) * 128, t0:t0 + n]),
                  writes=[wkB[s]], dma=wkB[s])
            P.add("act", lambda e, s=s, k=k, t0=t0, n=n: e.activation(
                out=aT.ap(k, t0, t0 + n), in_=wk[s][:, 0:n], func=AF.Identity,
                bias=der_ap(DER["B1_0"], k), scale=der_ap(DER["S1_0"], k)),
                reads=[wkB[s], B_der], writes=aT.bufs(k, t0, t0 + n))
        s = wslot()
        P.add("sp", lambda e, s=s, k=k: e.dma_start(out=wk[s][:, 0:NCTX], in_=ctxT[k * 128:(k + 1) * 128, :]),
              writes=[wkB[s]], dma=wkB[s])
        P.add("act", lambda e, s=s, k=k: e.activation(
            out=acT.ap(k), in_=wk[s][:, 0:NCTX], func=AF.Identity,
            bias=der_ap(DER["cB1"], k), scale=der_ap(DER["cS1"], k)),
            reads=[wkB[s], B_der], writes=acT.bufs(k))

    if stage <= 1:
        return finish(nc, P)
    o = 4096
    qT = View(Bq, o, 2, T); o += 2 * T
    kT = View(Bq, o, 2, TE); o += 2 * TE
    kcT = View(Bq, o, 2, NCTX); o += 2 * NCTX
    Vv = View(Bq, o, 18, 256); o += 18 * 256
    Vc = View(Bq, o, 2, 256); o += 2 * 256
    bia = [View(Bq, o + i * NPAT * 640, NPAT * 5, 128) for i in range(2)]; o += 2 * NPAT * 640
    PT = [View(Bq, o + i * 896, 7, 128) for i in range(2)]; o += 2 * 896
    OTst = [View(Bq, o + i * T, 1, T) for i in range(2)]; o += 2 * T
    assert o <= 32768, o
    SCALE = 128 ** -0.5

    def evac(pb, n, dst_ap, dst_bufs, scale=None, eng="dve"):
        if eng == "dve":
            if scale is None:
                P.add("dve", lambda e: e.tensor_copy(out=dst_ap, in_=ps[pb][:, 0:n]),
                      reads=[psB[pb]], writes=dst_bufs)
            else:
                P.add("dve", lambda e: e.tensor_scalar(out=dst_ap, in0=ps[pb][:, 0:n], scalar1=scale,
                                                       scalar2=None, op0=ALU.mult),
                      reads=[psB[pb]], writes=dst_bufs)
        else:
            if scale is None:
                P.add("act", lambda e: e.activation(out=dst_ap, in_=ps[pb][:, 0:n], func=AF.Copy),
                      reads=[psB[pb]], writes=dst_bufs)
            else:
                P.add("act", lambda e: e.activation(out=dst_ap, in_=ps[pb][:, 0:n], func=AF.Identity,
                                                    scale=scale),
                      reads=[psB[pb]], writes=dst_bufs)

    def attention_pair(hp):
        wqv = wload(w_qkv[:, hp * 256:(hp + 1) * 256], 256, KC, unit=0)
        wkv = wload(w_qkv[:, D + hp * 256:D + (hp + 1) * 256], 256, KC, unit=1)
        wvv = wload(w_qkv[:, 2 * D + hp * 256:2 * D + (hp + 1) * 256], 256, KC, unit=2)
        for hh in range(2):
            h = hp * 2 + hh
            bl = bia[hh].bufs3(0, NPAT * 5)
            P.add("pool", lambda e, hh=hh, h=h: e.dma_start(
                out=Bq.t[:, bia[hh].off:bia[hh].off + NPAT * 640], in_=biasT[h]), writes=bl, dma=bl[0])
        pbs = (6, 7)
        g = 0
        for hh in range(2):
            for t0 in range(0, T, 512):
                pb = pbs[g % 2]; g += 1
                gemm_group(pb, aT, wqv, hh * 128, t0, 512)
                evac(pb, 512, qT.ap(hh, t0, t0 + 512), qT.bufs(hh, t0, t0 + 512), scale=SCALE,
                     eng=("act" if g % 2 else "dve"))
            for t0 in range(0, TE, 512):
                n = min(512, TE - t0)
                pb = pbs[g % 2]; g += 1
                gemm_group(pb, aT, wkv, hh * 128, t0, n)
                evac(pb, n, kT.ap(hh, t0, t0 + n), kT.bufs(hh, t0, t0 + n), eng=("act" if g % 2 else "dve"))
            pb = pbs[g % 2]; g += 1
            gemm_group(pb, acT, wkv, hh * 128, 0, NCTX)
            evac(pb, NCTX, kcT.ap(hh), kcT.bufs(hh), eng=("act" if g % 2 else "dve"))
            ada_block()
        for tb in range(18 + 2):
            pb = pbs[g % 2]; g += 1
            for k in range(KC):
                if tb < 18:
                    lhs = aT.ap(k, tb * 128, tb * 128 + 128)
                    lb = aT.bufs(k, tb * 128, tb * 128 + 128)
                else:
                    lhs = acT.ap(k, (tb - 18) * 128, (tb - 18) * 128 + 128)
                    lb = acT.bufs(k, (tb - 18) * 128, (tb - 18) * 128 + 128)
                P.add("pe", lambda e, k=k, lhs=lhs, pb=pb: e.matmul(ps[pb][:, 0:256], lhsT=lhs, rhs=wvv.ap(k),
                                                                   start=(k == 0), stop=(k == KC - 1)),
                      reads=lb + wvv.bufs(k), writes=[psB[pb]])
            if tb < 18:
                evac(pb, 256, Vv.ap(tb), Vv.bufs(tb), eng=("act" if g % 2 else "dve"))
            else:
                evac(pb, 256, Vc.ap(tb - 18), Vc.bufs(tb - 18), eng=("act" if g % 2 else "dve"))
            if tb == 9:
                ada_block()

        for hh in range(2):
            h = hp * 2 + hh

            def qk(i, sset, hh=hh):
                kb0 = max(i - 2, 0)
                pat = min(i, NPAT - 1)
                for blk in range(7):
                    pb = sset * 2 + (0 if blk < 4 else 1)
                    c0 = (blk % 4) * 128
                    if blk < 5:
                        kt0 = (kb0 + blk) * 128
                        P.add("pe", lambda e, pb=pb, c0=c0, kt0=kt0: e.matmul(
                            ps[pb][:, c0:c0 + 128], lhsT=kT.ap(hh, kt0, kt0 + 128),
                            rhs=qT.ap(hh, i * 128, i * 128 + 128), start=True, stop=False),
                            reads=kT.bufs(hh, kt0, kt0 + 128) + qT.bufs(hh, i * 128, i * 128 + 128),
                            writes=[psB[pb]])
                        P.add("pe", lambda e, pb=pb, c0=c0, blk=blk, pat=pat: e.matmul(
                            ps[pb][:, c0:c0 + 128], lhsT=ident[:, :], rhs=bia[hh].ap(pat * 5 + blk),
                            start=False, stop=True),
                            reads=bia[hh].bufs(pat * 5 + blk) + [B_const], writes=[psB[pb]])
                    else:
                        cb = blk - 5
                        P.add("pe", lambda e, pb=pb, c0=c0, cb=cb: e.matmul(
                            ps[pb][:, c0:c0 + 128], lhsT=kcT.ap(hh, cb * 128, cb * 128 + 128),
                            rhs=qT.ap(hh, i * 128, i * 128 + 128), start=True, stop=True),
                            reads=kcT.bufs(hh, cb * 128, cb * 128 + 128) + qT.bufs(hh, i * 128, i * 128 + 128),
                            writes=[psB[pb]])

            def rest(i, sset, hh=hh):
                kb0 = max(i - 2, 0)
                pt = PT[sset]
                P.add("act", lambda e: e.activation(out=Bq.t[:, pt.off:pt.off + 512], in_=ps[sset * 2][:, 0:512],
                                                    func=AF.Exp),
                      reads=[psB[sset * 2]], writes=pt.bufs3(0, 4))
                P.add("act", lambda e: e.activation(out=Bq.t[:, pt.off + 512:pt.off + 896],
                                                    in_=ps[sset * 2 + 1][:, 0:384], func=AF.Exp),
                      reads=[psB[sset * 2 + 1]], writes=pt.bufs3(4, 7))
                half = i % 2
                pbo = 4 + half
                for blk in range(7):
                    if blk < 5:
                        vap = Vv.ap(kb0 + blk, hh * 128, hh * 128 + 128)
                        vb = Vv.bufs(kb0 + blk, hh * 128, hh * 128 + 128)
                    else:
                        vap = Vc.ap(blk - 5, hh * 128, hh * 128 + 128)
                        vb = Vc.bufs(blk - 5, hh * 128, hh * 128 + 128)
                    P.add("pe", lambda e, blk=blk, vap=vap: e.matmul(
                        ps[pbo][:, 0:128], lhsT=vap, rhs=pt.ap(blk), start=(blk == 0), stop=(blk == 6)),
                        reads=vb + pt.bufs(blk), writes=[psB[pbo]])
                for blk in range(7):
                    P.add("pe", lambda e, blk=blk: e.matmul(
                        ps[pbo][:, 128:256], lhsT=ones[:, :], rhs=pt.ap(blk), start=(blk == 0), stop=(blk == 6)),
                        reads=pt.bufs(blk) + [B_const], writes=[psB[pbo]])
                rd = st[:, half * 512:half * 512 + 128]
                P.add("dve", lambda e: e.reciprocal(out=rd, in_=ps[pbo][:, 128:256]),
                      reads=[psB[pbo]], writes=[B_st[half]])
                ot = OTst[hh]
                P.add("dve", lambda e: e.tensor_tensor(out=ot.ap(0, i * 128, i * 128 + 128),
                                                       in0=ps[pbo][:, 0:128], in1=rd, op=ALU.mult),
                      reads=[psB[pbo], B_st[half]], writes=ot.bufs(0, i * 128, i * 128 + 128))

            if not ATT:
                continue
            qk(0, 0)
            for i in range(NI):
                if i + 1 < NI:
                    qk(i + 1, (i + 1) % 2)
                rest(i, i % 2)
                if i in (4, 9, 14) or (i == 0 and hh == 1):
                    ada_block()
            ot = OTst[hh]
            wb = [B_OTs[h][t] for t in range(4)]
            P.add("sp", lambda e, ot=ot, h=h: e.dma_start(out=OTs[h * 128:(h + 1) * 128, :], in_=ot.ap(0)),
                  reads=ot.bufs(0), writes=wb, dma=ot.bufs(0)[0])

    for hp in range(NPAIR):
        attention_pair(hp)
    while ada_next[0] < NADA:
        ada_block()
    ada_finish()
    late_derived()
    if stage <= 2:
        return finish(nc, P)

    def load_A_from(src, Bsrc, arena, nk, tok0, ntok):
        v = View(arena, 0, nk, ntok)
        step = 4
        for k0 in range(0, nk, step):
            k1 = min(nk, k0 + step)
            bl = v.bufs3(k0, k1)
            rd = []
            for k in range(k0, k1):
                for t in range(tok0 // 512, (tok0 + ntok) // 512):
                    rd.append(Bsrc[k][t])
            srcap = src[k0 * 128:k1 * 128, tok0:tok0 + ntok].rearrange("(k p) t -> p k t", p=128)
            P.add("sp", lambda e, k0=k0, k1=k1, srcap=srcap: e.dma_start(out=v.ap3(k0, k1), in_=srcap),
                  reads=rd, writes=bl, dma=bl[0])
        return v

    def gemm_res_phase(Av, W, gs_name, res, Bres, dst, Bdst, pbs=(0, 1, 2, 3)):
        g = 0
        for blk in range(4):
            wv = wload(W[:, blk * 512:(blk + 1) * 512], 512, KC)
            for jj in range(4):
                fc = blk * 4 + jj
                for tt in range(4):
                    pb = pbs[g % len(pbs)]; g += 1
                    s = wslot()
                    P.add("sp", lambda e, s=s, fc=fc, tt=tt: e.dma_start(
                        out=wk[s][:, :], in_=res[fc * 128:(fc + 1) * 128, tt * 512:(tt + 1) * 512]),
                        reads=[Bres[fc][tt]] if Bres is not None else [], writes=[wkB[s]], dma=wkB[s])
                    gemm_group(pb, Av, wv, jj * 128, tt * 512, 512)
                    P.add("dve", lambda e, s=s, pb=pb, fc=fc: e.scalar_tensor_tensor(
                        out=wk[s][:, :], in0=ps[pb][:, :], scalar=der_ap(DER[gs_name], fc), in1=wk[s][:, :],
                        op0=ALU.mult, op1=ALU.add),
                        reads=[psB[pb], wkB[s], B_der], writes=[wkB[s]])
                    P.add("sp", lambda e, s=s, fc=fc, tt=tt: e.dma_start(
                        out=dst[fc * 128:(fc + 1) * 128, tt * 512:(tt + 1) * 512], in_=wk[s][:, :]),
                        reads=[wkB[s]], writes=[Bdst[fc][tt]], dma=wkB[s])

    def ln_phase(src, Bsrc, TW, eps, out_bf=None, bf_scale=None, bf_bias=None,
                 out_f32=None, Bout=None, f_scale=None, f_bias=None, scr_off=0, post=None,
                 pbs=(4, 5)):
        nt = T // TW
        nx = 2 * KC * TW
        sq = View(A, scr_off + 2 * nx, KC, TW)
        p1, p2 = pbs
        mean = st[:, 0:TW]
        rstd = st[:, 512:512 + TW]
        nmr = mean
        epsap = epsT[:, 0:1] if eps == EPS_RES else epsT[:, 1:2]

        def tile_views(ti):
            xoff = scr_off + (nx if ti % 2 else 0)
            xv = View(A, xoff, KC, 2 * TW)
            x32 = Af32[:, xoff // 2:xoff // 2 + KC * TW].rearrange("p (k t) -> p k t", t=TW)
            return xv, x32

        def stage_a(ti):
            t0 = ti * TW
            xv, x32 = tile_views(ti)
            rd = []
            for k in range(KC):
                for t in range(t0 // 512, (t0 + TW - 1) // 512 + 1):
                    rd.append(Bsrc[k][t])
            for k0 in range(0, KC, 4):
                gbufs = xv.bufs3(k0, k0 + 4)
                rdg = [Bsrc[k][t] for k in range(k0, k0 + 4) for t in range(t0 // 512, (t0 + TW - 1) // 512 + 1)]
                srcap = src[k0 * 128:(k0 + 4) * 128, t0:t0 + TW].rearrange("(k p) t -> p k t", p=128)
                P.add("sp", lambda e, k0=k0, srcap=srcap: e.dma_start(out=x32[:, k0:k0 + 4, :], in_=srcap),
                      reads=rdg, writes=gbufs, dma=gbufs[0])
            for k0 in range(0, KC, 4):
                P.add("act", lambda e, k0=k0: e.activation(out=sq.ap3(k0, k0 + 4), in_=x32[:, k0:k0 + 4, :],
                                                          func=AF.Square),
                      reads=xv.bufs3(k0, k0 + 4), writes=sq.bufs3(k0, k0 + 4))
            for k in range(KC):
                P.add("pe", lambda e, k=k: e.matmul(ps[p1][:, 0:TW], lhsT=onesf[:, :], rhs=x32[:, k, :],
                                                   start=(k == 0), stop=(k == KC - 1)),
                      reads=xv.bufs(k) + [B_const], writes=[psB[p1]])
            for k in range(KC):
                P.add("pe", lambda e, k=k: e.matmul(ps[p2][:, 0:TW], lhsT=ones[:, :], rhs=sq.ap(k),
                                                   start=(k == 0), stop=(k == KC - 1)),
                      reads=sq.bufs(k) + [B_const], writes=[psB[p2]])

        def stage_c(ti):
            P.add("dve", lambda e: e.tensor_scalar(out=mean, in0=ps[p1][:, 0:TW], scalar1=1.0 / D, scalar2=None,
                                                   op0=ALU.mult), reads=[psB[p1]], writes=[B_st[0]])
            P.add("dve", lambda e: e.tensor_tensor(out=rstd, in0=mean, in1=mean, op=ALU.mult),
                  reads=[B_st[0]], writes=[B_st[1]])
            P.add("dve", lambda e: e.scalar_tensor_tensor(out=rstd, in0=ps[p2][:, 0:TW], scalar=1.0 / D, in1=rstd,
                                                          op0=ALU.mult, op1=ALU.subtract),
                  reads=[psB[p2], B_st[1]], writes=[B_st[1]])
            P.add("act", lambda e: e.activation(out=rstd, in_=rstd, func=AF.Sqrt, bias=epsap, scale=1.0),
                  reads=[B_st[1], B_const], writes=[B_st[1]])
            P.add("dve", lambda e: e.reciprocal(out=rstd, in_=rstd), reads=[B_st[1]], writes=[B_st[1]])
            P.add("dve", lambda e: e.scalar_tensor_tensor(out=nmr, in0=mean, scalar=-1.0, in1=rstd,
                                                          op0=ALU.mult, op1=ALU.mult),
                  reads=[B_st[0], B_st[1]], writes=[B_st[0]])

        def stage_b(ti):
            t0 = ti * TW
            xv, x32 = tile_views(ti)
            G = 4
            for k0 in range(0, KC, G):
                xg = x32[:, k0:k0 + G, :]
                gb = xv.bufs3(k0, k0 + G)
                rb = rstd.unsqueeze(1).to_broadcast([128, G, TW])
                nb = nmr.unsqueeze(1).to_broadcast([128, G, TW])
                P.add("dve", lambda e, xg=xg, rb=rb: e.tensor_tensor(out=xg, in0=xg, in1=rb, op=ALU.mult),
                      reads=gb + [B_st[1]], writes=gb)
                P.add("dve", lambda e, xg=xg, nb=nb: e.tensor_tensor(out=xg, in0=xg, in1=nb, op=ALU.add),
                      reads=gb + [B_st[0]], writes=gb)
                for k in range(k0, k0 + G):
                    xk = x32[:, k, :]
                    kb = xv.bufs(k)
                    if out_f32 is not None:
                        if k % 4 != 3:
                            P.add("dve", lambda e, xk=xk, k=k: e.tensor_scalar(
                                out=xk, in0=xk, scalar1=f_scale(k), scalar2=f_bias(k), op0=ALU.mult, op1=ALU.add),
                                reads=kb + [B_small], writes=kb)
                        else:
                            P.add("act", lambda e, xk=xk, k=k: e.activation(
                                out=xk, in_=xk, func=AF.Identity, bias=f_bias(k), scale=f_scale(k)),
                                reads=kb + [B_small], writes=kb)
                    if out_bf is not None:
                        P.add("act", lambda e, xk=xk, k=k: e.activation(
                            out=out_bf.ap(k, t0, t0 + TW), in_=xk, func=AF.Identity,
                            bias=bf_bias(k), scale=bf_scale(k)),
                            reads=kb + [B_der, B_small, B_adaP], writes=out_bf.bufs(k, t0, t0 + TW))
                if out_f32 is not None:
                    wr = [Bout[k][t] for k in range(k0, k0 + G) for t in range(t0 // 512, (t0 + TW - 1) // 512 + 1)]
                    dstap = out_f32[k0 * 128:(k0 + G) * 128, t0:t0 + TW].rearrange("(k p) t -> p k t", p=128)
                    P.add("sp", lambda e, k0=k0, dstap=dstap: e.dma_start(out=dstap, in_=x32[:, k0:k0 + G, :]),
                          reads=gb, writes=wr, dma=gb[0])

        stage_a(0)
        stage_c(0)
        for ti in range(nt):
            if ti + 1 < nt:
                stage_a(ti + 1)
            stage_b(ti)
            if ti + 1 < nt:
                stage_c(ti + 1)
            if post is not None:
                post(ti, ti * TW)

    Af32 = A.t[:, :].bitcast(F32)

    def ffn(l, hT):
        Win = ffn_w_in[l]
        Wout = ffn_w_out[l]
        g = [0]

        def ffn_in_group(wg, wu, cc, c, tt):
            pg, pu = ((0, 1), (2, 3))[g[0] % 2]; g[0] += 1
            gemm_group(pg, hT, wg, cc * 128, tt * 512, 512)
            gemm_group(pu, hT, wu, cc * 128, tt * 512, 512)
            s = wslot()
            P.add("act", lambda e: e.activation(out=wk[s][:, :], in_=ps[pg][:, :], func=AF.Silu),
                  reads=[psB[pg]], writes=[wkB[s]])
            hs = hslot()
            P.add("dve", lambda e: e.tensor_tensor(out=hb[hs][:, :], in0=ps[pu][:, :], in1=wk[s][:, :], op=ALU.mult),
                  reads=[psB[pu], wkB[s]], writes=[hbB[hs]])
            P.add("sp", lambda e: e.dma_start(out=Gs[c * 128:(c + 1) * 128, tt * 512:(tt + 1) * 512], in_=hb[hs][:, :]),
                  reads=[hbB[hs]], writes=[B_Gs[c][tt]], dma=hbB[hs])

        def ffn_w(hbk):
            return (wload(Win[:, hbk * 256:(hbk + 1) * 256], 256, KC),
                    wload(Win[:, DFF + hbk * 256:DFF + (hbk + 1) * 256], 256, KC))

        w0 = ffn_w(0)
        w1 = ffn_w(1)
        for tt in range(4):
            for hbk, (wg, wu) in ((0, w0), (1, w1)):
                for cc in range(2):
                    ffn_in_group(wg, wu, cc, hbk * 2 + cc, tt)
        for hbk in range(2, FC // 2):
            wg, wu = ffn_w(hbk)
            for cc in range(2):
                for tt in range(4):
                    ffn_in_group(wg, wu, cc, hbk * 2 + cc, tt)
        gname = f"g2_{l}"
        for th in range(2):
            gv = load_A_from(Gs, B_Gs, A, FC, th * 1024, 1024)
            for fb in range(8):
                wA = wload(Wout[0:22 * 128, fb * 256:(fb + 1) * 256], 256, 22)
                wB = wload(Wout[22 * 128:44 * 128, fb * 256:(fb + 1) * 256], 256, 22)
                pset = (0, 1, 2, 3) if fb % 2 == 0 else (4, 5, 6, 7)
                grp = [(jj, t2) for jj in range(2) for t2 in range(2)]
                slots = []
                for gi, (jj, t2) in enumerate(grp):
                    fc = fb * 2 + jj
                    tt = th * 2 + t2
                    s = wslot()
                    slots.append(s)
                    P.add("sp", lambda e, s=s, fc=fc, tt=tt: e.dma_start(
                        out=wk[s][:, :], in_=Hres[fc * 128:(fc + 1) * 128, tt * 512:(tt + 1) * 512]),
                        reads=[B_Hres[fc][tt]], writes=[wkB[s]], dma=wkB[s])
                for gi, (jj, t2) in enumerate(grp):
                    gemm_group(pset[gi], gv, wA, jj * 128, t2 * 512, 512, koff=0, start=True, stop=False)
                for gi, (jj, t2) in enumerate(grp):
                    gemm_group(pset[gi], gv, wB, jj * 128, t2 * 512, 512, koff=22, start=False, stop=True)
                for gi, (jj, t2) in enumerate(grp):
                    fc = fb * 2 + jj
                    tt = th * 2 + t2
                    s = slots[gi]
                    pb = pset[gi]
                    P.add("dve", lambda e, s=s, pb=pb, fc=fc: e.scalar_tensor_tensor(
                        out=wk[s][:, :], in0=ps[pb][:, :], scalar=der_ap(DER[gname], fc), in1=wk[s][:, :],
                        op0=ALU.mult, op1=ALU.add),
                        reads=[psB[pb], wkB[s], B_der], writes=[wkB[s]])
                    P.add("sp", lambda e, s=s, fc=fc, tt=tt: e.dma_start(
                        out=Hpre[fc * 128:(fc + 1) * 128, tt * 512:(tt + 1) * 512], in_=wk[s][:, :]),
                        reads=[wkB[s]], writes=[B_Hpre[fc][tt]], dma=wkB[s])

    def lnp_col(l, wh, gb):
        base = ((l * 2 + wh) * 2 + gb) * KC
        return lambda k: lnps[:, base + k:base + k + 1]

    def der_col(name):
        return lambda k: der_ap(DER[name], k)

    Bx = None
    OTv = load_A_from(OTs, B_OTs, A, KC, 0, T)
    gemm_res_phase(OTv, w_o[:, :], "g1_0", xT, None, Hpre, B_Hpre)
    if stage <= 3:
        return finish(nc, P)
    hT = View(Bq, 0, KC, T)
    ln_phase(Hpre, B_Hpre, 512, EPS_RES, out_bf=hT, bf_scale=der_col("S2_0"), bf_bias=(lambda k: adaP[:, (0 * 96 + 3 * KC + k) * 2:(0 * 96 + 3 * KC + k) * 2 + 1]),
             out_f32=Hres, Bout=B_Hres, f_scale=lnp_col(0, 0, 0), f_bias=lnp_col(0, 0, 1))
    if stage <= 4:
        return finish(nc, P)
    ffn(0, hT)
    if stage <= 5:
        return finish(nc, P)
    ln_phase(Hpre, B_Hpre, 512, EPS_RES, out_bf=hT, bf_scale=der_col("S1_1"), bf_bias=(lambda k: adaP[:, (1 * 96 + 0 * KC + k) * 2:(1 * 96 + 0 * KC + k) * 2 + 1]),
             out_f32=Hres, Bout=B_Hres, f_scale=lnp_col(0, 1, 0), f_bias=lnp_col(0, 1, 1))
    if stage <= 6:
        return finish(nc, P)

    uT = View(A, 0, KC, T)
    g = [0]

    def gm_group(wv, jj, fc, tt):
        pb = (0, 1, 2, 3)[g[0] % 4]; g[0] += 1
        gemm_group(pb, hT, wv, jj * 128, tt * 512, 512)
        if fc < KC:
            P.add("act", lambda e: e.activation(out=uT.ap(fc, tt * 512, tt * 512 + 512), in_=ps[pb][:, :],
                                                func=AF.Gelu),
                  reads=[psB[pb]], writes=uT.bufs(fc, tt * 512, tt * 512 + 512))
        else:
            s = wslot()
            P.add("act", lambda e: e.activation(out=wk[s][:, :], in_=ps[pb][:, :], func=AF.Gelu),
                  reads=[psB[pb]], writes=[wkB[s]])
            P.add("sp", lambda e: e.dma_start(
                out=Vs[(fc - KC) * 128:(fc - KC + 1) * 128, tt * 512:(tt + 1) * 512], in_=wk[s][:, :]),
                reads=[wkB[s]], writes=[B_Vs[fc - KC][tt]], dma=wkB[s])

    gw = [wload(gm_w_in[:, blk * 512:(blk + 1) * 512], 512, KC) for blk in range(2)]
    for tt in range(4):
        for blk in range(2):
            for jj in range(4):
                gm_group(gw[blk], jj, blk * 4 + jj, tt)
    for blk in range(2, 8):
        wv = wload(gm_w_in[:, blk * 512:(blk + 1) * 512], 512, KC)
        for jj in range(4):
            for tt in range(4):
                gm_group(wv, jj, blk * 4 + jj, tt)
    if stage <= 7:
        return finish(nc, P)

    SCR = 32768
    vn = View(Wa, 4096, KC, 128)
    vT = View(Wa, 12288, KC, 128)
    wsv = View(Wa, 0, KC, 128)
    P.add("pool", lambda e: e.dma_start(out=Wa.t[:, 0:2048], in_=w_sT), writes=[Wa.bufs[0]], dma=Wa.bufs[0])
    Wf32 = Wa.t[:, :].bitcast(F32)
    bsv = Wf32[:, 4096:4096 + 2048]
    P.add("pool", lambda e: e.dma_start(out=bsv, in_=bsb), writes=Wa.bufs[2:3], dma=Wa.bufs[2])
    psT = [ps[6][:, :].bitcast(BF16), ps[7][:, :].bitcast(BF16)]
    zT = View(Bq, 0, KC, T)

    def spatial(ti, t0):
        for half in range(2):
            for gi in range(8):
                gg = half * 8 + gi
                P.add("pe", lambda e, gg=gg, gi=gi, half=half: e.transpose(
                    out=psT[half][:, gi * 128:(gi + 1) * 128], in_=vn.ap(gg), identity=ident[:, :]),
                    reads=vn.bufs(gg) + [B_const], writes=[psB[6 + half]])
            P.add("dve" if half == 0 else "act",
                  (lambda e, half=half: e.tensor_copy(out=vT.ap3(half * 8, half * 8 + 8), in_=psT[half].rearrange(
                      "p (g c) -> p g c", c=128))) if half == 0 else
                  (lambda e, half=half: e.activation(out=vT.ap3(half * 8, half * 8 + 8), in_=psT[half].rearrange(
                      "p (g c) -> p g c", c=128), func=AF.Copy)),
                  reads=[psB[6 + half]], writes=vT.bufs3(half * 8, half * 8 + 8))
        for q4 in range(4):
            pb = q4
            for gi in range(4):
                gg = q4 * 4 + gi
                P.add("pe", lambda e, gg=gg, gi=gi, pb=pb: e.matmul(
                    ps[pb][:, gi * 128:(gi + 1) * 128], lhsT=vT.ap(gg), rhs=wsv.ap(gg), start=True, stop=True),
                    reads=vT.bufs(gg) + [Wa.bufs[0]], writes=[psB[pb]])
            s = wslot()
            P.add("dve", lambda e, s=s, pb=pb, q4=q4: e.tensor_tensor(
                out=wk[s][:, :], in0=ps[pb][:, :], in1=bsv[:, q4 * 512:(q4 + 1) * 512], op=ALU.add),
                reads=[psB[pb]] + Wa.bufs[2:3], writes=[wkB[s]])
            P.add("pool", lambda e, s=s, q4=q4, t0=t0: e.tensor_tensor(
                out=zT.ap3(q4 * 4, q4 * 4 + 4, t0, t0 + 128),
                in0=wk[s][:, :].rearrange("p (g c) -> p g c", c=128),
                in1=uT.ap3(q4 * 4, q4 * 4 + 4, t0, t0 + 128), op=ALU.mult),
                reads=[wkB[s]] + uT.bufs3(q4 * 4, q4 * 4 + 4, t0, t0 + 128),
                writes=zT.bufs3(q4 * 4, q4 * 4 + 4, t0, t0 + 128))

    ln_phase(Vs, B_Vs, 128, LN_EPS, out_bf=_VNView(vn), bf_scale=lambda k: gmlns[:, k:k + 1],
             bf_bias=lambda k: gmlns[:, KC + k:KC + k + 1], scr_off=SCR, post=spatial, pbs=(4, 5))
    if stage <= 8:
        return finish(nc, P)
    gemm_res_phase(zT, gm_w_out[:, :], "g1_1", Hres, B_Hres, Hpre, B_Hpre)
    ln_phase(Hpre, B_Hpre, 512, EPS_RES, out_bf=hT, bf_scale=der_col("S2_1"), bf_bias=(lambda k: adaP[:, (1 * 96 + 3 * KC + k) * 2:(1 * 96 + 3 * KC + k) * 2 + 1]),
             out_f32=Hres, Bout=B_Hres, f_scale=lnp_col(1, 0, 0), f_bias=lnp_col(1, 0, 1))
    if stage <= 9:
        return finish(nc, P)
    ffn(1, hT)
    ln_phase(Hpre, B_Hpre, 512, EPS_RES, out_f32=outT, Bout=B_out, f_scale=lnp_col(1, 1, 0),
             f_bias=lnp_col(1, 1, 1))
    return finish(nc, P)


class _VNView:
    def __init__(self, vn):
        self.vn = vn

    def ap(self, k, lo, hi):
        return self.vn.ap(k, 0, hi - lo)

    def bufs(self, k, lo, hi):
        return self.vn.bufs(k, 0, hi - lo)


LASTP = None


def finish(nc, P):
    global LASTP
    LASTP = P
    with nc.Block() as block:
        P.emit(block)
    return nc


def _ext_rows(half):
    return list(range(36)) if half == 0 else list(range(63, 27, -1))


def _bias_tables(rpb, half):
    rows = np.asarray(_ext_rows(half))
    out = np.full((NH, 128, NPAT, 5, 128), MASKV, np.float32)
    cq = np.arange(64)
    c0 = np.clip(cq - 8, 0, 48)
    for pat in range(NPAT):
        i = pat
        kb0 = max(i - 2, 0)
        for lq in range(2):
            gq = rows[2 * i + lq]
            r0 = int(np.clip(gq - 4, 0, 56))
            for blk in range(5):
                for lk in range(2):
                    gk = rows[2 * (kb0 + blk) + lk]
                    if not (r0 <= gk < r0 + 8):
                        continue
                    dr = gk - gq + 7
                    ck = np.arange(64)
                    valid = (ck[:, None] >= c0[None, :]) & (ck[:, None] < c0[None, :] + 16)
                    dc = ck[:, None] - cq[None, :] + 15
                    vals = rpb[:, dr, :][:, np.clip(dc, 0, 30)]
                    blkv = np.where(valid[None], vals, MASKV)
                    out[:, lk * 64:(lk + 1) * 64, pat, blk, lq * 64:(lq + 1) * 64] = blkv
    return np.ascontiguousarray(out.reshape(NH, 128, NPAT * 5 * 128))


def _pp(v):
    v = np.asarray(v, np.float32)
    lead = v.shape[:-1]
    n = v.shape[-1] // 128
    a = v.reshape(lead + (n, 128))
    return np.ascontiguousarray(np.moveaxis(a, -1, 0))


def make_in_maps(x, c, ctx, c_ctx, ada_w, ada_b, ln_g, ln_b, na_w_qkv, na_w_o, na_rpb,
                 gm_w_in, gm_ln_g, gm_ln_b, gm_w_s, gm_b_s, gm_w_out, ffn_w_in, ffn_w_out, cores=range(8)):
    f = lambda a: np.ascontiguousarray(np.asarray(a, dtype=np.float32))
    x = f(x); ctx = f(ctx)
    shared = dict(
        ada_w=f(ada_w), w_qkv=f(na_w_qkv[0]), w_o=f(na_w_o[0]), gm_w_in=f(gm_w_in[0]),
        gm_w_out=f(gm_w_out[0]), ffn_w_in=f(ffn_w_in), ffn_w_out=f(ffn_w_out),
    )
    shared["identin"] = np.eye(128, dtype=np.float32)
    ab = _pp(f(ada_b))
    shared["adab"] = np.ascontiguousarray(np.repeat(ab[:, :, :, None], 2, axis=3).reshape(128, -1))
    lg = _pp(f(ln_g)); lb = _pp(f(ln_b))
    shared["lnp"] = np.ascontiguousarray(np.stack([lg, lb], axis=3).reshape(128, -1))
    shared["gmln"] = np.ascontiguousarray(
        np.stack([_pp(f(gm_ln_g[0])), _pp(f(gm_ln_b[0]))], axis=1).reshape(128, -1))
    rpb = f(na_rpb[0])
    bias_half = [_bias_tables(rpb, 0), _bias_tables(rpb, 1)]
    ws = f(gm_w_s[0])
    bs = f(gm_b_s[0])
    perm = (np.arange(128) + 64) % 128
    wsT = [None, None]
    bsbb = [None, None]
    for half in range(2):
        w = ws if half == 0 else ws[:, perm][:, :, perm]
        b = bs if half == 0 else bs[:, perm]
        wsT[half] = np.ascontiguousarray(np.transpose(w, (2, 0, 1)).reshape(128, -1))
        bsbb[half] = np.ascontiguousarray(np.broadcast_to(b.reshape(1, -1), (128, KC * 128)))
    maps = []
    for core in cores:
        b, half = core // 2, core % 2
        rows = _ext_rows(half)
        xe = x[b].reshape(64, 64, D)[rows].reshape(TE, D)
        m = dict(shared)
        m["xT"] = np.ascontiguousarray(xe.T)
        m["ctxT"] = np.ascontiguousarray(ctx[b].T)
        cvv = np.stack([_pp(f(c[b])), _pp(f(c_ctx))], axis=2)
        m["cvec"] = np.ascontiguousarray(cvv.reshape(128, -1))
        m["biasT"] = bias_half[half]
        m["w_sT"] = wsT[half]
        m["bsb"] = bsbb[half]
        maps.append(m)
    return maps


def assemble(outs, cores=range(8)):
    out = np.zeros((4, 64, 64, D), np.float32)
    for core, oT in zip(cores, outs):
        b, half = core // 2, core % 2
        rows = _ext_rows(half)[:32]
        out[b, rows] = np.ascontiguousarray(oT.T).reshape(32, 64, D)
    return out.reshape(4, 4096, D)


_NC = None


def kernel(**inputs):
    global _NC
    if _NC is None:
        _NC = build()
    in_maps = make_in_maps(**inputs)
    res = run_bass_kernel_spmd(_NC, in_maps, core_ids=list(range(8)))
    return assemble([r["outT"] for r in res.results])
```
